# Optimizing a Trainium2 kernel written in Bass

```python
import jax, jax.numpy as jnp
from jax import lax
import numpy as np


D_MODEL = 2048
BATCH = 4
SEQ = 4096
DEPTH = 1

GRID_W = 64
CTX_LEN = 256
MIX_WIDTH = D_MODEL
FOURIER_WIDTH = MIX_WIDTH // 2
FOURIER_GROUPS = 4
FOURIER_GROUP_DIM = FOURIER_WIDTH // FOURIER_GROUPS
MLSTM_WIDTH = MIX_WIDTH - FOURIER_WIDTH
MLSTM_HEADS = 4
MLSTM_HEAD_DIM = MLSTM_WIDTH // MLSTM_HEADS
N_DIRS = 2
GATE_COLS = N_DIRS * 2 * MLSTM_HEADS
PROJ_WIDTH = FOURIER_WIDTH + 4 * MLSTM_WIDTH + GATE_COLS
CONV_K = 5
CHUNK = 64
D_FF = 4 * D_MODEL
N_MOD = 6
EPS = 1e-6
POS_BASE = 10000.0

kernel_name = 'hymba_fnet_bimlstm_dit_block'


def rms_norm(x, g):
    xf = x.astype(jnp.float32)
    y = xf * lax.rsqrt(jnp.mean(xf * xf, axis=-1, keepdims=True) + EPS)
    return (y * g.astype(jnp.float32)).astype(x.dtype)


def modulate(h, shift, scale):
    return h * (1.0 + scale) + shift


def sincos_pos_2d(rows, dtype):
    quarter = D_MODEL // 4
    omega = 1.0 / (POS_BASE ** (jnp.arange(quarter, dtype=jnp.float32) / quarter))
    ar = jnp.arange(rows, dtype=jnp.float32)[:, None] * omega
    ac = jnp.arange(GRID_W, dtype=jnp.float32)[:, None] * omega
    row_emb = jnp.broadcast_to(jnp.concatenate([jnp.sin(ar), jnp.cos(ar)], -1)[:, None, :], (rows, GRID_W, 2 * quarter))
    col_emb = jnp.broadcast_to(jnp.concatenate([jnp.sin(ac), jnp.cos(ac)], -1)[None, :, :], (rows, GRID_W, 2 * quarter))
    return jnp.concatenate([row_emb, col_emb], -1).reshape(rows * GRID_W, D_MODEL).astype(dtype)


def centred_dwconv(u, w):
    K, C = w.shape
    return lax.conv_general_dilated(u, w[:, None, :].astype(u.dtype), window_strides=(1,),
                                    padding=[(K // 2, K // 2)],
                                    dimension_numbers=('NWC', 'WIO', 'NWC'),
                                    feature_group_count=C)


def split_proj(p):
    cuts = [FOURIER_WIDTH + i * MLSTM_WIDTH for i in range(5)]
    return jnp.split(p, cuts, axis=-1)


def to_heads(a):
    B, T, _ = a.shape
    return a.reshape(B, T, MLSTM_HEADS, MLSTM_HEAD_DIM).transpose(0, 2, 1, 3)


def mlstm_inputs(p_q, p_k, p_v, p_g, conv_w, gate_b):
    qk = jax.nn.silu(centred_dwconv(jnp.concatenate([p_q, p_k], axis=-1), conv_w))
    q, k = jnp.split(qk, 2, axis=-1)
    q = to_heads(q).astype(jnp.float32) * (MLSTM_HEAD_DIM ** -0.5)
    k = to_heads(k).astype(jnp.float32)
    v = to_heads(p_v).astype(jnp.float32)
    B, T, _ = p_g.shape
    g = p_g.reshape(B, T, N_DIRS, 2, MLSTM_HEADS).astype(jnp.float32) + gate_b.astype(jnp.float32)
    g = g.transpose(2, 3, 0, 4, 1)
    log_i = g[:, 0]
    log_f = jax.nn.log_sigmoid(g[:, 1])
    return q, k, v, log_i, log_f


def mlstm_chunkwise(q, k, v, log_i, log_f, state):
    B, H, T, _ = q.shape
    L = CHUNK
    nc = T // L

    def to_chunks(a):
        return jnp.moveaxis(a.reshape(B, H, nc, L, *a.shape[3:]), 2, 0)

    xs = tuple(to_chunks(a) for a in (q, k, v, log_i, log_f))
    lower = jnp.tril(jnp.ones((L, L), dtype=bool))

    def step(carry, inp):
        C, n, m = carry
        qc, kc, vc, li, lf = inp
        b = jnp.cumsum(lf, axis=-1)
        d_mat = jnp.where(lower, b[..., :, None] - b[..., None, :] + li[..., None, :], -jnp.inf)
        inter = b + m[..., None]
        m_t = jnp.maximum(inter, jnp.max(d_mat, axis=-1))
        a = jnp.exp(inter - m_t)
        s = jnp.einsum('bhtd,bhsd->bhts', qc, kc) * jnp.exp(d_mat - m_t[..., None])
        num = a[..., None] * jnp.einsum('bhtd,bhde->bhte', qc, C) + jnp.einsum('bhts,bhse->bhte', s, vc)
        den = a * jnp.einsum('bhtd,bhd->bht', qc, n) + jnp.sum(s, axis=-1)
        h = num / jnp.maximum(jnp.abs(den), jnp.exp(-m_t))[..., None]
        b_end = b[..., -1]
        g = b_end[..., None] - b + li
        m_new = jnp.maximum(b_end + m, jnp.max(g, axis=-1))
        decay = jnp.exp(b_end + m - m_new)
        kw = kc * jnp.exp(g - m_new[..., None])[..., None]
        C_new = decay[..., None, None] * C + jnp.einsum('bhsd,bhse->bhde', kw, vc)
        n_new = decay[..., None] * n + jnp.sum(kw, axis=2)
        return (C_new, n_new, m_new), h

    state, hs = lax.scan(step, state, xs)
    return jnp.moveaxis(hs, 0, 2).reshape(B, H, T, v.shape[-1]), state


def mlstm_final_state(k, v, log_i, log_f):
    b = jnp.cumsum(log_f, axis=-1)
    b_end = b[..., -1]
    g = b_end[..., None] - b + log_i
    m = jnp.maximum(b_end, jnp.max(g, axis=-1))
    kw = k * jnp.exp(g - m[..., None])[..., None]
    return (jnp.einsum('bhsd,bhse->bhde', kw, v), jnp.sum(kw, axis=2), m)


def mlstm_direction(q, k, v, log_i, log_f, state, reverse):
    if reverse:
        q, k, v, log_i, log_f = (jnp.flip(a, axis=2) for a in (q, k, v, log_i, log_f))
    h, _ = mlstm_chunkwise(q, k, v, log_i, log_f, state)
    return jnp.flip(h, axis=2) if reverse else h


def context_state(k, v, log_i, log_f, reverse):
    if reverse:
        k, v, log_i, log_f = (jnp.flip(a, axis=2) for a in (k, v, log_i, log_f))
    return mlstm_final_state(k, v, log_i, log_f)


def fourier_mix(u):
    B, T, _ = u.shape
    ug = u.reshape(B, T, FOURIER_GROUPS, FOURIER_GROUP_DIM).astype(jnp.float32)
    y = jnp.fft.fft2(ug, axes=(1, 3), norm='ortho').real
    return y.reshape(B, T, FOURIER_WIDTH).astype(u.dtype)


def mixer_output(f, h, o, head_g, w_out):
    B, H, T, dh = h.shape
    hn = rms_norm(h.transpose(0, 2, 1, 3), head_g.reshape(H, dh)).reshape(B, T, H * dh)
    ym = hn * jax.nn.sigmoid(o.astype(jnp.float32))
    y = jnp.concatenate([fourier_mix(f), ym.astype(f.dtype)], axis=-1)
    return y @ w_out


def squared_relu_mlp(h, w1, w2):
    return jnp.square(jax.nn.relu(h @ w1)) @ w2


def hybrid_layer(x, ctx, mod_x, mod_c, g_mix, g_mlp, w_in, conv_w, gate_b, head_g, w_out,
                 w_mlp1, w_mlp2, update_ctx):
    sh1, sc1, gt1, sh2, sc2, gt2 = (m[:, None, :] for m in jnp.split(mod_x, N_MOD, axis=-1))
    csh1, csc1, cgt1, csh2, csc2, cgt2 = jnp.split(mod_c, N_MOD, axis=-1)
    hx = modulate(rms_norm(x, g_mix), sh1, sc1)
    hc = modulate(rms_norm(ctx, g_mix), csh1, csc1)
    fx, qx, kx, vx, ox, gx = split_proj(hx @ w_in)
    fc, qc, kc, vc, oc, gc = split_proj(hc @ w_in)
    q_x, k_x, v_x, li_x, lf_x = mlstm_inputs(qx, kx, vx, gx, conv_w, gate_b)
    q_c, k_c, v_c, li_c, lf_c = mlstm_inputs(qc, kc, vc, gc, conv_w, gate_b)
    st_fwd = context_state(k_c, v_c, li_c[0], lf_c[0], False)
    st_bwd = context_state(k_c, v_c, li_c[1], lf_c[1], True)
    h_x = (mlstm_direction(q_x, k_x, v_x, li_x[0], lf_x[0], st_fwd, False)
           + mlstm_direction(q_x, k_x, v_x, li_x[1], lf_x[1], st_bwd, True))
    x = x + gt1 * mixer_output(fx, h_x, ox, head_g, w_out)
    x = x + gt2 * squared_relu_mlp(modulate(rms_norm(x, g_mlp), sh2, sc2), w_mlp1, w_mlp2)
    if update_ctx:
        B, H, _, dh = k_c.shape
        zero = (jnp.zeros((B, H, dh, dh), jnp.float32), jnp.zeros((B, H, dh), jnp.float32),
                jnp.zeros((B, H), jnp.float32))
        h_c = (mlstm_direction(q_c, k_c, v_c, li_c[0], lf_c[0], zero, False)
               + mlstm_direction(q_c, k_c, v_c, li_c[1], lf_c[1], zero, True))
        ctx = ctx + cgt1 * mixer_output(fc, h_c, oc, head_g, w_out)
        ctx = ctx + cgt2 * squared_relu_mlp(modulate(rms_norm(ctx, g_mlp), csh2, csc2), w_mlp1, w_mlp2)
    return x, ctx


def setup_inputs(seed: int = 0) -> dict:
    key = jax.random.key(seed)
    ks = jax.random.split(key, 16)

    def nrm(k, shape, scale):
        return jax.random.normal(k, shape, jnp.float32) * scale

    gate_base = jnp.stack([jnp.zeros((MLSTM_HEADS,), jnp.float32),
                           jnp.linspace(3.0, 6.0, MLSTM_HEADS, dtype=jnp.float32)])
    return {
        'x': nrm(ks[0], (BATCH, SEQ, D_MODEL), 1.0),
        'c': nrm(ks[1], (BATCH, D_MODEL), 1.0),
        'ctx': nrm(ks[2], (BATCH, CTX_LEN, D_MODEL), 1.0),
        'c_ctx': nrm(ks[3], (D_MODEL,), 1.0),
        'w_mod': nrm(ks[4], (DEPTH, D_MODEL, N_MOD * D_MODEL), 0.5 * D_MODEL ** -0.5),
        'b_mod': nrm(ks[5], (DEPTH, N_MOD * D_MODEL), 0.02),
        'g_mix': 1.0 + nrm(ks[6], (DEPTH, D_MODEL), 0.02),
        'g_mlp': 1.0 + nrm(ks[7], (DEPTH, D_MODEL), 0.02),
        'w_in': nrm(ks[8], (DEPTH, D_MODEL, PROJ_WIDTH), D_MODEL ** -0.5),
        'conv_w': nrm(ks[9], (DEPTH, CONV_K, 2 * MLSTM_WIDTH), CONV_K ** -0.5),
        'gate_b': gate_base[None, None] + nrm(ks[10], (DEPTH, N_DIRS, 2, MLSTM_HEADS), 0.1),
        'head_g': 1.0 + nrm(ks[11], (DEPTH, MLSTM_WIDTH), 0.02),
        'w_out': nrm(ks[12], (DEPTH, MIX_WIDTH, D_MODEL), MIX_WIDTH ** -0.5),
        'w_mlp1': nrm(ks[13], (DEPTH, D_MODEL, D_FF), D_MODEL ** -0.5),
        'w_mlp2': nrm(ks[14], (DEPTH, D_FF, D_MODEL), D_FF ** -0.5),
        'g_final': 1.0 + nrm(ks[15], (D_MODEL,), 0.02),
    }


def reference(x, c, ctx, c_ctx, w_mod, b_mod, g_mix, g_mlp, w_in, conv_w, gate_b, head_g, w_out,
              w_mlp1, w_mlp2, g_final):
    n_tokens = x.shape[1]
    rows = n_tokens // GRID_W
    x = x + sincos_pos_2d(rows, x.dtype)[None]
    for l in range(DEPTH):
        mod_x = jax.nn.silu(c) @ w_mod[l] + b_mod[l]
        mod_c = jax.nn.silu(c_ctx) @ w_mod[l] + b_mod[l]
        x, ctx = hybrid_layer(x, ctx, mod_x, mod_c, g_mix[l], g_mlp[l], w_in[l], conv_w[l], gate_b[l],
                              head_g[l], w_out[l], w_mlp1[l], w_mlp2[l], update_ctx=(l < DEPTH - 1))
    return rms_norm(x, g_final)
```

```python
import numpy as np
import ml_dtypes
from contextlib import ExitStack
import concourse.bass as bass
import concourse.mybir as mybir
from concourse.bass_utils import run_bass_kernel_spmd

F32 = mybir.dt.float32
BF16 = mybir.dt.bfloat16
AF = mybir.ActivationFunctionType
ALU = mybir.AluOpType
NPBF = ml_dtypes.bfloat16

D = 2048; T = 4096; TC = 256; NTS = 32; NTT = 34; OWN = 16
EPS = 1e-6


class Buf:
    __slots__ = ("name", "w", "r")

    def __init__(self, name):
        self.name = name; self.w = None; self.r = {}


class KB:
    def __init__(self, nc, es):
        self.nc = nc; self.es = es
        self.E = {"pe": nc.tensor, "act": nc.scalar, "dve": nc.vector, "pool": nc.gpsimd, "sp": nc.sync}
        self.sem = {k: es.enter_context(nc.semaphore("s_" + k)) for k in ["pe", "act", "dve", "pool"]}
        self.cnt = {k: 0 for k in self.sem}
        self.waited = {k: {} for k in self.E}
        self.dsems = {}
        self.bar = es.enter_context(nc.semaphore("s_bar")); self.barn = 0
        self.nobar = {"wcast"}

    def _wait(self, eng, toks):
        for (key, sem, val, src, kind) in toks:
            if src == eng and kind != "raw":
                continue
            if self.waited[eng].get(key, 0) >= val:
                continue
            self.E[eng].wait_ge(sem, val); self.waited[eng][key] = val

    def _deps(self, reads, writes):
        toks = []
        for b in reads:
            if b.w: toks.append(b.w + ("raw",))
        for b in writes:
            if b.w: toks.append(b.w + ("waw",))
            for t in b.r.values(): toks.append(t + ("war",))
        return toks

    def _post(self, tok, reads, writes):
        for b in reads:
            o = b.r.get(tok[0])
            if o is None or o[2] < tok[2]: b.r[tok[0]] = tok
        for b in writes:
            b.w = tok; b.r = {}

    def op(self, eng, fn, reads=(), writes=(), inc=True):
        self._wait(eng, self._deps(reads, writes))
        ins = fn()
        if inc:
            self.cnt[eng] += 1; ins.then_inc(self.sem[eng], 1)
            tok = (eng, self.sem[eng], self.cnt[eng], eng)
        else:
            tok = (eng, self.sem[eng], self.cnt[eng] + 1, eng)
        self._post(tok, reads, writes)
        return ins

    def dma(self, q, out, in_, reads, writes, semname):
        self._wait(q, self._deps(reads, writes))
        if semname not in self.dsems:
            self.dsems[semname] = [self.es.enter_context(self.nc.semaphore("d_" + semname)), 0]
        ent = self.dsems[semname]
        ins = self.E[q].dma_start(out=out, in_=in_)
        ent[1] += 16; ins.then_inc(ent[0], 16)
        tok = ("d_" + semname, ent[0], ent[1], None)
        self._post(tok, reads, writes)

    def barrier(self):
        sp = self.E["sp"]
        for k in self.sem:
            if self.cnt[k] > self.waited["sp"].get(k, 0):
                sp.wait_ge(self.sem[k], self.cnt[k]); self.waited["sp"][k] = self.cnt[k]
        for name, (sem, val) in self.dsems.items():
            if name in self.nobar: continue
            if val > self.waited["sp"].get("d_" + name, 0):
                sp.wait_ge(sem, val); self.waited["sp"]["d_" + name] = val
        self.barn += 1
        sp.sem_inc(self.bar, 1)
        for k in ["pe", "act", "dve", "pool"]:
            self.E[k].wait_ge(self.bar, self.barn)
            for k2 in self.sem: self.waited[k][k2] = max(self.waited[k].get(k2, 0), self.cnt[k2])
            for name, (sem, val) in self.dsems.items():
                if name in self.nobar: continue
                self.waited[k]["d_" + name] = val

    def final_wait(self):
        sp = self.E["sp"]
        for name, (sem, val) in self.dsems.items():
            if val > self.waited["sp"].get("d_" + name, 0):
                sp.wait_ge(sem, val)
        for k in self.sem:
            if self.cnt[k] > self.waited["sp"].get(k, 0):
                sp.wait_ge(self.sem[k], self.cnt[k])


def build_program(debug=False):
    nc = bass.Bass("TRN2", target_bir_lowering=False)
    es = ExitStack()
    kb = KB(nc, es)
    E = kb.E

    def din(name, shape, dt=F32):
        return nc.dram_tensor(name, list(shape), dt, kind="ExternalInput").ap()

    def dscr(name, shape, dt):
        return nc.dram_tensor(name, list(shape), dt, kind="Internal").ap()

    x_d = din("x", [T, D]); ctx_d = din("ctx", [TC, D]); pos_d = din("pos", [T, D])
    ccT_d = din("ccT", [128, 16, 2])
    wmod_d = din("w_mod", [D, 6 * D]); bmodfm_d = din("bmod_fm", [128, 96]); bmodbc_d = din("bmod_bc", [128, 2, D])
    gmix_d = din("gmix_fm", [128, 16]); gmlp_d = din("gmlp_fm", [128, 16])
    win_d = din("w_in", [D, 5136]); convfm_d = din("conv_fm", [128, 16, 5]); gateb_d = din("gateb_bc", [128, 16])
    headg_d = din("headg_bc", [128, 1024]); gfin_d = din("gfin_bc", [128, D])
    wout_d = din("w_out", [D, D]); w1_d = din("w_mlp1", [D, 4 * D]); w2_d = din("w_mlp2", [4 * D, D])
    csd_d = din("csd", [128, 2, 512], BF16)
    tabc_d = din("tabc", [16, 128, 32, 128], BF16); tabs_d = din("tabs", [16, 128, 32, 128], BF16)
    cst_d = din("cst", [128, 5, 128])
    out_d = nc.dram_tensor("out", [2048, D], F32, kind="ExternalOutput").ap()
    A_d = dscr("A_scr", [2, NTS, 128, 1024], BF16)
    qkpre_d = dscr("qkpre_scr", [16, 128, NTT * 128], BF16)
    qT_d = dscr("qT_scr", [8, 128, 2048], BF16)
    kT_d = dscr("kT_scr", [8, 128, NTT * 128], BF16)
    ktm_d = dscr("ktm_scr", [NTT * 128, 1024], BF16)
    vtm_d = dscr("vtm_scr", [NTT * 128, 1024], BF16)
    otm_d = dscr("otm_scr", [2048, 1024], BF16)
    hA_d = dscr("hA_scr", [2048, 1024], F32)
    x1_d = dscr("x1_scr", [2048, D], F32)
    yT_d = dscr("yT_scr", [OWN, 128, 16, 128], BF16)
    gt_d = dscr("gt_scr", [2, 128, D], F32)
    w1b_d = dscr("w1b_scr", [D, 4 * D], BF16)
    w2b_d = dscr("w2b_scr", [4 * D, D], BF16)
    h2T_d = dscr("h2T_scr", [4, 128, 16, 512], BF16)
    dbg = {}
    if debug:
        dbg["x1"] = nc.dram_tensor("dbg_x1", [2048, D], F32, kind="ExternalOutput").ap()
        dbg["mod"] = nc.dram_tensor("dbg_mod", [128, 96, 2], F32, kind="ExternalOutput").ap()
        dbg["gates"] = nc.dram_tensor("dbg_gates", [128, NTT, 16], F32, kind="ExternalOutput").ap()

    def sb(stack, name, shape, dt):
        return stack.enter_context(nc.sbuf_tensor("sb_" + name, list(shape), dt))

    def ps(stack, name, shape, dt):
        return stack.enter_context(nc.psum_tensor("ps_" + name, list(shape), dt))

    cst = sb(es, "cst", [128, 5, 128], F32); b_cst = Buf("cst")
    cstb = sb(es, "cstb", [128, 5, 128], BF16); b_cstb = Buf("cstb")
    modfm = sb(es, "modfm", [128, 96, 2], F32); b_modfm = Buf("modfm")
    g1 = sb(es, "g1", [128, 16, 2], F32); b_g1 = Buf("g1")
    g2 = sb(es, "g2", [128, 16], F32); b_g2 = Buf("g2")
    b_gtd = [Buf("gtd0"), Buf("gtd1")]
    b_w1b = Buf("w1b"); b_w2b = Buf("w2b")
    gtm = sb(es, "gtm", [128, NTT, 16], F32); b_gtm = Buf("gtm")
    wS = sb(es, "wS", [128, NTT, 8], F32); rhoS = sb(es, "rhoS", [128, NTT, 8], F32); decS = sb(es, "decS", [128, NTT, 8], F32)
    b_wS = Buf("wS"); b_rhoS = Buf("rhoS"); b_decS = Buf("decS")
    b_yT = [Buf("yT%d" % i) for i in range(OWN)]
    ph1 = ExitStack()
    pbank = [ps(ph1, "pb%d" % i, [128, 512], F32) for i in range(6)]
    pbank_b = [ps(ph1, "pbb%d" % i, [128, 1024], BF16) for i in range(2)]
    b_pb = [Buf("pb%d" % i) for i in range(6)]; b_pbb = [Buf("pbb0"), Buf("pbb1")]

    maskA = cstb[:, 0, :]; maskB = cstb[:, 1, :]; identb = cstb[:, 2, :]
    triA = cst[:, 0, :]; triB = cst[:, 1, :]; onesf = cst[:, 3, :]

    kb.dma("sp", cst[:], cst_d, [], [b_cst], "cst")
    kb.op("dve", lambda: E["dve"].tensor_copy(cstb[:], cst[:]), [b_cst], [b_cstb])

    wmod_v = wmod_d.rearrange("(k p) n -> p k n", p=128)

    def stage0(stack, pcs, first):
        ccT = sb(stack, "ccT%d" % first, [128, 16, 2], F32); b_ccT = Buf("ccT")
        scT = sb(stack, "scT%d" % first, [128, 16, 2], BF16); b_scT = Buf("scT")
        bmfm = sb(stack, "bmfm%d" % first, [128, 96], F32); b_bmfm = Buf("bmfm")
        gmx = sb(stack, "gmx%d" % first, [128, 16], F32); b_gm = Buf("gm")
        wp = [sb(stack, "wmp%d_%d" % (first, i), [128, 16, 512], BF16) for i in range(2)]; b_wp = [Buf("wmp0"), Buf("wmp1")]
        sfx = "_%d" % first
        kb.dma("sp", ccT[:], ccT_d, [], [b_ccT], "ccT" + sfx)
        kb.dma("sp", bmfm[:], bmodfm_d, [], [b_bmfm], "bmfm" + sfx)
        kb.dma("sp", gmx[:], gmix_d if first else gmlp_d, [], [b_gm], "gmx" + sfx)
        kb.op("act", lambda: E["act"].activation(out=scT[:], in_=ccT[:], func=AF.Silu), [b_ccT], [b_scT])
        if not first:
            scR = sb(stack, "scR", [128, 16, 128], BF16); b_scR = Buf("scR")
            bmbc = sb(stack, "bmbc", [128, 2, D], F32); b_bmbc = Buf("bmbc")
            gts = [sb(stack, "gts%d" % i, [128, 512], F32) for i in range(2)]; b_gts = [Buf("gts0"), Buf("gts1")]
            kb.dma("sp", bmbc[:], bmodbc_d, [], [b_bmbc], "bmbc")
            for k in range(16):
                kb.op("act", lambda k=k: E["act"].activation(out=scR[:, k, :], in_=ccT[:, k, 0:1].to_broadcast([128, 128]), func=AF.Silu), [b_ccT], [b_scR])
        gi_ = 0
        for n_, pc in enumerate(pcs):
            blk = pc // 4
            w = wp[n_ % 2]; bw = b_wp[n_ % 2]
            kb.dma("pool", w[:], wmod_v[:, :, pc * 512:(pc + 1) * 512], [], [bw], "wmp%d%s" % (n_ % 2, sfx))
            if blk in (0, 1, 3, 4):
                for fcn in range(4):
                    ch = pc * 4 + fcn
                    for k in range(16):
                        kb.op("pe", lambda k=k, fcn=fcn, w=w: E["pe"].matmul(pbank[0][:, 0:2], w[:, k, fcn * 128:(fcn + 1) * 128], scT[:, k, :], start=(k == 0), stop=(k == 15)),
                              [bw, b_scT], [b_pb[0]], inc=(k == 15))
                    kb.op("dve", lambda ch=ch: E["dve"].tensor_tensor(modfm[:, ch, :], pbank[0][:, 0:2], bmfm[:, ch:ch + 1].to_broadcast([128, 2]), ALU.add),
                          [b_pb[0], b_bmfm], [b_modfm])
            else:
                gi = 0 if blk == 2 else 1
                col = (pc % 4) * 512
                for k in range(16):
                    kb.op("pe", lambda k=k, w=w: E["pe"].matmul(pbank[1][:, :], scR[:, k, :], w[:, k, :], start=(k == 0), stop=(k == 15)),
                          [bw, b_scR], [b_pb[1]], inc=(k == 15))
                g_ = gts[gi_ % 2]; bg = b_gts[gi_ % 2]; gname = "gts%d" % (gi_ % 2); gi_ += 1
                kb.op("dve", lambda gi=gi, col=col, g_=g_: E["dve"].tensor_tensor(g_[:], pbank[1][:, :], bmbc[:, gi, col:col + 512], ALU.add),
                      [b_pb[1], b_bmbc], [bg])
                kb.dma("sp", gt_d[gi, :, col:col + 512], g_[:], [bg], [b_gtd[gi]], gname)
            yield
        if first:
            kb.op("dve", lambda: E["dve"].tensor_scalar(g1[:], modfm[:, 16:32, :], 1.0, None, ALU.add), [b_modfm], [b_g1])
            kb.op("dve", lambda: E["dve"].tensor_tensor(g1[:, :, 0], g1[:, :, 0], gmx[:], ALU.mult), [b_g1, b_gm], [b_g1])
            kb.op("dve", lambda: E["dve"].tensor_tensor(g1[:, :, 1], g1[:, :, 1], gmx[:], ALU.mult), [b_g1, b_gm], [b_g1])
        else:
            kb.op("dve", lambda: E["dve"].tensor_scalar(g2[:], modfm[:, 64:80, 0], 1.0, None, ALU.add), [b_modfm], [b_g2])
            kb.op("dve", lambda: E["dve"].tensor_tensor(g2[:], g2[:], gmx[:], ALU.mult), [b_g2, b_gm], [b_g2])
        yield

    with ExitStack() as s0:
        for _ in stage0(s0, list(range(8)), 1):
            pass
        kb.barrier()

    def load_tile_src(ti):
        if ti < NTS:
            return x_d[ti * 128:(ti + 1) * 128, :], pos_d[ti * 128:(ti + 1) * 128, :]
        return ctx_d[(ti - NTS) * 128:(ti - NTS + 1) * 128, :], None

    win_v = win_d.rearrange("(k p) n -> p k n", p=128)
    with ExitStack() as sab:
        hxT = sb(sab, "hxT", [128, 16, 18 * 128], BF16)
        b_hx = [Buf("hx%d" % i) for i in range(18)]
        xt = [sb(sab, "xt%d" % i, [128, D], F32) for i in range(2)]; b_xt = [Buf("xt0"), Buf("xt1")]
        pt = [sb(sab, "pt%d" % i, [128, D], F32) for i in range(2)]; b_pt = [Buf("pt0"), Buf("pt1")]
        xn = sb(sab, "xn", [128, D], BF16); b_xn = Buf("xn")
        junk = sb(sab, "junk", [128, D], BF16); b_junk = Buf("junk")
        st = sb(sab, "st", [128, 4], F32); b_st = Buf("st")
        wpc = [sb(sab, "wpc%d" % i, [128, 16, 512], BF16) for i in range(2)]; b_wpc = [Buf("wpc0"), Buf("wpc1")]
        wg = sb(sab, "wg", [128, 16, 16], BF16); b_wg = Buf("wg")
        csd = sb(sab, "csd", [128, 2, 512], BF16); b_csd = Buf("csd")
        uT = sb(sab, "uT", [128, 4, 512], BF16); b_uT = [Buf("uT%d" % i) for i in range(4)]
        ast = [sb(sab, "ast%d" % i, [128, 2, 2, 256], BF16) for i in range(4)]; b_ast = [Buf("ast%d" % i) for i in range(4)]
        stg = [sb(sab, "stg%d" % i, [128, 512], BF16) for i in range(4)]; b_stg = [Buf("stg%d" % i) for i in range(4)]
        kb.dma("sp", csd[:], csd_d, [], [b_csd], "csd")
        kb.dma("pool", wg[:], win_v[:, :, 5120:5136], [], [b_wg], "wg")
        stgi = [0]; pbi = [0]

        def next_stg():
            i = stgi[0] % 4; stgi[0] += 1; return i

        def next_pb():
            i = 2 + (pbi[0] % 4); pbi[0] += 1; return i

        for half in range(2):
            tiles = list(range(0, 16)) if half == 0 else list(range(16, 34))
            for si, ti in enumerate(tiles):
                xs, psrc = load_tile_src(ti)
                r = 0 if ti < NTS else 1
                xb = xt[si % 2]; bxb = b_xt[si % 2]
                kb.dma("sp", xb[:], xs, [], [bxb], "xt%d" % (si % 2))
                if psrc is not None:
                    pb_ = pt[si % 2]; bpb = b_pt[si % 2]
                    kb.dma("sp", pb_[:], psrc, [], [bpb], "pt%d" % (si % 2))
                    kb.op("dve", lambda xb=xb, pb_=pb_: E["dve"].tensor_tensor(xb[:], xb[:], pb_[:], ALU.add), [bxb, bpb], [bxb])
                kb.op("act", lambda xb=xb: E["act"].activation(out=junk[:], in_=xb[:], func=AF.Square, accum_out=st[:, 0:1]), [bxb], [b_junk, b_st])
                kb.op("act", lambda: E["act"].activation(out=st[:, 1:2], in_=st[:, 0:1], func=AF.Ln, scale=1.0 / D, bias=EPS), [b_st], [b_st])
                kb.op("act", lambda: E["act"].activation(out=st[:, 2:3], in_=st[:, 1:2], func=AF.Exp, scale=-0.5), [b_st], [b_st])
                kb.op("act", lambda xb=xb: E["act"].activation(out=xn[:], in_=xb[:], func=AF.Copy, scale=st[:, 2:3]), [bxb, b_st], [b_xn])
                for hb in range(2):
                    for j in range(8):
                        jj = hb * 8 + j
                        kb.op("pe", lambda jj=jj, j=j, hb=hb: E["pe"].transpose(pbank_b[hb][:, j * 128:(j + 1) * 128], xn[:, jj * 128:(jj + 1) * 128], identb),
                              [b_xn, b_cstb], [b_pbb[hb]], inc=(j == 7))
                    for j in range(8):
                        jj = hb * 8 + j
                        kb.op("dve", lambda jj=jj, j=j, hb=hb, si=si, r=r: E["dve"].tensor_scalar(
                            hxT[:, jj, si * 128:(si + 1) * 128], pbank_b[hb][:, j * 128:(j + 1) * 128], g1[:, jj, r:r + 1], modfm[:, jj, r:r + 1], ALU.mult, ALU.add),
                            [b_pbb[hb], b_g1, b_modfm], [b_hx[si]])
            nt = len(tiles)
            blocks = [(b0, min(4, nt - b0)) for b0 in range(0, nt, 4)]
            pieces = []
            pieces += [("f", 0, 0), ("f", 512, 1)]
            pieces += [("q", 1024, 0), ("q", 1536, 1)]
            pieces += [("k", 2048, 0), ("k", 2560, 1)]
            pieces += [("v", 3072, 0), ("v", 3584, 1)]
            if half == 0:
                pieces += [("o", 4096, 0), ("o", 4608, 1)]
            for pi, (kind, col0, idx) in enumerate(pieces):
                w = wpc[pi % 2]; bw = b_wpc[pi % 2]
                kb.dma("pool", w[:], win_v[:, :, col0:col0 + 512], [], [bw], "wpc%d" % (pi % 2))
                for (b0, nb) in blocks:
                    ncol = nb * 128
                    tiles_b = tiles[b0:b0 + nb]
                    is_ctx = tiles_b[0] >= NTS
                    hxbufs = [b_hx[b0 + i] for i in range(nb)]
                    if kind == "f":
                        if is_ctx: continue
                        for cc in range(4):
                            pbn = next_pb()
                            for k in range(16):
                                kb.op("pe", lambda k=k, cc=cc, w=w, pbn=pbn, b0=b0, ncol=ncol: E["pe"].matmul(pbank[pbn][:, 0:ncol], w[:, k, cc * 128:(cc + 1) * 128], hxT[:, k, b0 * 128:b0 * 128 + ncol], start=(k == 0), stop=(k == 15)),
                                      [bw] + hxbufs, [b_pb[pbn]], inc=(k == 15))
                            kb.op("act", lambda cc=cc, pbn=pbn, ncol=ncol: E["act"].activation(out=uT[:, cc, 0:ncol], in_=pbank[pbn][:, 0:ncol], func=AF.Copy), [b_pb[pbn]], [b_uT[cc]])
                        for i in range(nb):
                            ti = tiles_b[i]
                            ai = ti % 4
                            for g in range(2):
                                pbn = next_pb()
                                for jc in range(2):
                                    kb.op("pe", lambda g=g, jc=jc, i=i, pbn=pbn: E["pe"].matmul(pbank[pbn][:, :], uT[:, g * 2 + jc, i * 128:(i + 1) * 128], csd[:, jc, :], start=(jc == 0), stop=(jc == 1)),
                                          [b_uT[g * 2 + jc], b_csd], [b_pb[pbn]], inc=(jc == 1))
                                kb.op("dve", lambda g=g, pbn=pbn, ai=ai: E["dve"].tensor_copy(ast[ai][:, :, g, :], pbank[pbn][:, :].rearrange("p (c m) -> p c m", c=2)), [b_pb[pbn]], [b_ast[ai]])
                            kb.dma("sp", A_d[idx, ti], ast[ai][:].rearrange("p a b c -> p (a b c)"), [b_ast[ai]], [], "ast%d" % ai)
                    elif kind in ("q", "k"):
                        if kind == "q" and not (half == 0 or b0 == 0): continue
                        for cc in range(4):
                            pbn = next_pb()
                            for k in range(16):
                                kb.op("pe", lambda k=k, cc=cc, w=w, pbn=pbn, b0=b0, ncol=ncol: E["pe"].matmul(pbank[pbn][:, 0:ncol], w[:, k, cc * 128:(cc + 1) * 128], hxT[:, k, b0 * 128:b0 * 128 + ncol], start=(k == 0), stop=(k == 15)),
                                      [bw] + hxbufs, [b_pb[pbn]], inc=(k == 15))
                            si_ = next_stg()
                            kb.op("act", lambda pbn=pbn, ncol=ncol, si_=si_: E["act"].activation(out=stg[si_][:, 0:ncol], in_=pbank[pbn][:, 0:ncol], func=AF.Copy), [b_pb[pbn]], [b_stg[si_]])
                            chn = (0 if kind == "q" else 8) + idx * 4 + cc
                            t0 = tiles_b[0] * 128
                            kb.dma("sp", qkpre_d[chn, :, t0:t0 + ncol], stg[si_][:, 0:ncol], [b_stg[si_]], [], "stg%d" % si_)
                    else:
                        for i in range(nb):
                            ti = tiles_b[i]
                            pbn = next_pb()
                            for k in range(16):
                                kb.op("pe", lambda k=k, w=w, pbn=pbn, b0=b0, i=i: E["pe"].matmul(pbank[pbn][:, :], hxT[:, k, (b0 + i) * 128:(b0 + i + 1) * 128], w[:, k, :], start=(k == 0), stop=(k == 15)),
                                      [bw, b_hx[b0 + i]], [b_pb[pbn]], inc=(k == 15))
                            si_ = next_stg()
                            kb.op("act", lambda pbn=pbn, si_=si_: E["act"].activation(out=stg[si_][:], in_=pbank[pbn][:, :], func=AF.Copy), [b_pb[pbn]], [b_stg[si_]])
                            dst = vtm_d if kind == "v" else otm_d
                            kb.dma("sp", dst[ti * 128:(ti + 1) * 128, idx * 512:(idx + 1) * 512], stg[si_][:], [b_stg[si_]], [], "stg%d" % si_)
            for si, ti in enumerate(tiles):
                pbn = next_pb()
                for k in range(16):
                    kb.op("pe", lambda k=k, pbn=pbn, si=si: E["pe"].matmul(pbank[pbn][:, 0:16], hxT[:, k, si * 128:(si + 1) * 128], wg[:, k, :], start=(k == 0), stop=(k == 15)),
                          [b_wg, b_hx[si]], [b_pb[pbn]], inc=(k == 15))
                kb.op("dve", lambda pbn=pbn, ti=ti: E["dve"].tensor_copy(gtm[:, ti, :], pbank[pbn][:, 0:16]), [b_pb[pbn]], [b_gtm])
        kb.barrier()

    with ExitStack() as sc:
        cvf = sb(sc, "cvf", [128, 16, 5], F32); b_cvf = Buf("cvf")
        dg = sb(sc, "dg", [128, 5, 128], BF16); b_dg = Buf("dg")
        pre = [sb(sc, "pre%d" % i, [128, NTT * 128], BF16) for i in range(2)]; b_pre = [Buf("pre0"), Buf("pre1")]
        cs_ = [sb(sc, "cs%d" % i, [128, 512], BF16) for i in range(3)]; b_cs = [Buf("cs%d" % i) for i in range(3)]
        kt_ = [sb(sc, "kt%d" % i, [128, 512], BF16) for i in range(2)]; b_kt = [Buf("kts0"), Buf("kts1")]
        kb.dma("sp", cvf[:], convfm_d, [], [b_cvf], "cvf")
        for i in range(16):
            kb.dma("pool", w1b_d[i * 128:(i + 1) * 128, :], w1_d[i * 128:(i + 1) * 128, :], [], [b_w1b], "wcast")
        for i in range(16):
            kb.dma("pool", w2b_d[i * 512:(i + 1) * 512, :], w2_d[i * 512:(i + 1) * 512, :], [], [b_w2b], "wcast")
        gen0b = stage0(sc, list(range(8, 24)), 0)
        cit = 0
        ci = 0; pbc = 0; kti = 0
        for chn in range(16):
            isq = chn < 8
            ntok = 2560 if isq else NTT * 128
            p_ = pre[chn % 2]; bp = b_pre[chn % 2]
            kb.dma("sp", p_[:, 0:ntok], qkpre_d[chn, :, 0:ntok], [], [bp], "pre%d" % (chn % 2))
            for j in range(5):
                kb.op("dve", lambda j=j, chn=chn: E["dve"].tensor_scalar(dg[:, j, :], identb, cvf[:, chn, j:j + 1], None, ALU.mult), [b_cstb, b_cvf], [b_dg])
            segs = [(0, T, 0, 2048 if isq else T)]
            if not isq: segs.append((T, T + TC, T, T + TC))
            for (slo, shi, olo, ohi) in segs:
                for t0 in range(olo, ohi, 512):
                    n = min(512, ohi - t0)
                    pbn = 2 + (pbc % 4); pbc += 1
                    order = [2, 0, 1, 3, 4]
                    for oi, j in enumerate(order):
                        a = max(t0, slo - (j - 2)); b = min(t0 + n, shi - (j - 2))
                        kb.op("pe", lambda j=j, a=a, b=b, t0=t0, pbn=pbn, p_=p_, oi=oi: E["pe"].matmul(pbank[pbn][:, a - t0:b - t0], dg[:, j, :], p_[:, a + j - 2:b + j - 2], start=(oi == 0), stop=(oi == 4)),
                              [b_dg, bp], [b_pb[pbn]], inc=(oi == 4))
                    cit += 1
                    if cit % 4 == 0: next(gen0b, None)
                    c_ = cs_[ci % 3]; bc = b_cs[ci % 3]; cname = "cs%d" % (ci % 3); ci += 1
                    kb.op("act", lambda pbn=pbn, n=n, c_=c_: E["act"].activation(out=c_[:, 0:n], in_=pbank[pbn][:, 0:n], func=AF.Silu), [b_pb[pbn]], [bc])
                    if isq:
                        kb.dma("sp", qT_d[chn, :, t0:t0 + n], c_[:, 0:n], [bc], [], cname)
                    else:
                        kb.dma("sp", kT_d[chn - 8, :, t0:t0 + n], c_[:, 0:n], [bc], [], cname)
                        nb = n // 128
                        hb = kti % 2
                        for i in range(nb):
                            kb.op("pe", lambda i=i, hb=hb, c_=c_: E["pe"].transpose(pbank_b[hb][:, i * 128:(i + 1) * 128], c_[:, i * 128:(i + 1) * 128], identb),
                                  [bc, b_cstb], [b_pbb[hb]], inc=(i == nb - 1))
                        k_ = kt_[kti % 2]; bk = b_kt[kti % 2]; kname = "kts%d" % (kti % 2); kti += 1
                        kb.op("dve", lambda hb=hb, k_=k_, n=n: E["dve"].tensor_copy(k_[:, 0:n], pbank_b[hb][:, 0:n]), [b_pbb[hb]], [bk])
                        kb.dma("sp", ktm_d[t0:t0 + n, (chn - 8) * 128:(chn - 7) * 128].rearrange("(i p) c -> p i c", p=128),
                               k_[:, 0:n].rearrange("p (i c) -> p i c", c=128), [bk], [], kname)
        for _ in gen0b:
            pass
        kb.barrier()

    with ExitStack() as sd:
        gb = sb(sd, "gb", [128, 16], F32); b_gb = Buf("gb")
        z = sb(sd, "z", [128, NTT, 16], F32); b_z = Buf("z")
        lf = sb(sd, "lf", [128, 2, NTT, 4], F32); b_lf = Buf("lf")
        bb = sb(sd, "bb", [128, 2, NTT, 4], F32); b_bb = Buf("bb")
        kb.dma("sp", gb[:], gateb_d, [], [b_gb], "gb")
        for ti in range(NTT):
            kb.op("dve", lambda ti=ti: E["dve"].tensor_tensor(z[:, ti, :], gtm[:, ti, :], gb[:], ALU.add), [b_gtm, b_gb], [b_z])
        for d in range(2):
            kb.op("act", lambda d=d: E["act"].activation(out=lf[:, d], in_=z[:, :, d * 8 + 4:d * 8 + 8], func=AF.Exp, scale=-1.0), [b_z], [b_lf])
        kb.op("act", lambda: E["act"].activation(out=lf[:], in_=lf[:], func=AF.Ln, bias=1.0), [b_lf], [b_lf])
        kb.op("dve", lambda: E["dve"].tensor_scalar(lf[:], lf[:], -1.0, None, ALU.mult), [b_lf], [b_lf])
        for d in range(2):
            tri = triA if d == 0 else triB
            kb.op("pe", lambda d=d, tri=tri: E["pe"].matmul(pbank[0][:, d * 136:(d + 1) * 136], tri, lf[:, d].rearrange("p t c -> p (t c)"), start=True, stop=True), [b_lf, b_cst], [b_pb[0]])
            kb.op("pe", lambda d=d: E["pe"].matmul(pbank[1][:, d * 136:(d + 1) * 136], onesf, lf[:, d].rearrange("p t c -> p (t c)"), start=True, stop=True), [b_lf, b_cst], [b_pb[1]])
        kb.op("dve", lambda: E["dve"].tensor_copy(bb[:].rearrange("p d t c -> p (d t c)"), pbank[0][:, 0:272]), [b_pb[0]], [b_bb])
        for d in range(2):
            kb.op("dve", lambda d=d: E["dve"].tensor_tensor(wS[:, :, d * 4:d * 4 + 4], z[:, :, d * 8:d * 8 + 4], bb[:, d], ALU.subtract), [b_z, b_bb], [b_wS])
            kb.op("act", lambda d=d: E["act"].activation(out=rhoS[:, :, d * 4:d * 4 + 4], in_=bb[:, d], func=AF.Exp, scale=-1.0, bias=float(np.log(16.0))), [b_bb], [b_rhoS])
            kb.op("act", lambda d=d: E["act"].activation(out=decS[:, :, d * 4:d * 4 + 4], in_=pbank[1][:, d * 136:(d + 1) * 136].rearrange("p (t c) -> p t c", c=4), func=AF.Exp), [b_pb[1]], [b_decS])
        kb.op("act", lambda: E["act"].activation(out=wS[:], in_=wS[:], func=AF.Exp), [b_wS], [b_wS])
        if debug:
            kb.dma("sp", dbg["gates"], gtm[:], [b_gtm], [], "dbggates")
        kb.barrier()
    ph1.close()
    ph2 = ExitStack()
    pbank = [ps(ph2, "qb%d" % i, [128, 512], F32) for i in range(7)]
    pbank_b = [ps(ph2, "qbb0", [128, 1024], BF16)]
    b_pb = [Buf("qb%d" % i) for i in range(7)]; b_pbb = [Buf("qbb0")]

    with ExitStack() as se:
        qTh = sb(se, "qTh", [128, 2, 2048], BF16); kTh = sb(se, "kTh", [128, 2, NTT * 128], BF16)
        ktmh = sb(se, "ktmh", [128, NTT, 256], BF16); vtmh = sb(se, "vtmh", [128, NTT, 256], BF16)
        b_q = Buf("hq"); b_k = Buf("hk"); b_kt = Buf("hkt"); b_v = Buf("hv")
        b_hAd = {}
        hgb = sb(se, "hgb", [128, 1024], F32); b_hgb = Buf("hgb")
        Sf = sb(se, "Sf", [128, 2, 2, 257], F32); Sb = sb(se, "Sb", [128, 2, 2, 2, 257], BF16)
        b_Sf = [Buf("Sf0"), Buf("Sf1")]; b_Sb = [[Buf("Sb00"), Buf("Sb01")], [Buf("Sb10"), Buf("Sb11")]]
        tmpS = sb(se, "tmpS", [128, 2, 2, 257], F32); b_tmpS = [Buf("tmpS0"), Buf("tmpS1")]
        Vp = [sb(se, "Vp%d" % i, [128, 257], BF16) for i in range(4)]; b_Vp = [Buf("Vp%d" % i) for i in range(4)]
        STm = [sb(se, "STm%d" % i, [128, 128], BF16) for i in range(2)]; b_STm = [Buf("STm0"), Buf("STm1")]
        sm = [sb(se, "sm%d" % i, [128, 8], F32) for i in range(2)]; b_sm = [Buf("sm0"), Buf("sm1")]
        hAst = [sb(se, "hAst%d" % i, [128, 256], F32) for i in range(2)]; b_hAst = [Buf("hAst0"), Buf("hAst1")]
        hAld = [sb(se, "hAld%d" % i, [128, 256], F32) for i in range(2)]; b_hAld = [Buf("hAld0"), Buf("hAld1")]
        ot = [sb(se, "ot%d" % i, [128, 256], BF16) for i in range(2)]; b_ot = [Buf("ot0"), Buf("ot1")]
        hh = sb(se, "hh", [128, 256], F32); b_hh = Buf("hh")
        sg = sb(se, "sg", [128, 256], F32); b_sg = Buf("sg")
        ym = sb(se, "ym", [128, 256], BF16); b_ym = Buf("ym")
        junk2 = sb(se, "junk2", [128, 256], BF16); b_junk2 = Buf("junk2")
        yts = [sb(se, "yts%d" % i, [128, 2, 128], BF16) for i in range(2)]; b_yts = [Buf("yts0"), Buf("yts1")]
        Asb = sb(se, "Asb", [128, NTS, 1024], BF16); b_A = Buf("Asb")
        tc_ = [sb(se, "tc%d" % i, [128, 32, 128], BF16) for i in range(2)]; ts_ = [sb(se, "ts%d" % i, [128, 32, 128], BF16) for i in range(2)]
        b_tc = [Buf("tc0"), Buf("tc1")]; b_ts = [Buf("ts0"), Buf("ts1")]
        yst = [sb(se, "yst%d" % i, [128, 512], BF16) for i in range(2)]; b_yst = [Buf("yst0"), Buf("yst1")]
        yst2 = [sb(se, "ystb%d" % i, [128, 4, 128], BF16) for i in range(2)]; b_yst2 = [Buf("ystb0"), Buf("ystb1")]
        kb.dma("sp", hgb[:], headg_d, [], [b_hgb], "hgb")

        def gen_F():
            it = 0
            for pas in range(2):
                for q4 in range(4):
                    kb.dma("sp", Asb[:, q4 * 8:(q4 + 1) * 8, :], A_d[pas, q4 * 8:(q4 + 1) * 8].rearrange("t p c -> p t c"), [], [b_A], "Asb")
                for kch in range(16):
                    i2 = it % 2; it += 1
                    kb.dma("sp", tc_[i2][:], tabc_d[kch], [], [b_tc[i2]], "tc%d" % i2)
                    kb.dma("sp", ts_[i2][:], tabs_d[kch], [], [b_ts[i2]], "ts%d" % i2)
                    for tt in range(32):
                        kb.op("pe", lambda tt=tt, i2=i2: E["pe"].matmul(pbank[6][:, :], tc_[i2][:, tt, :], Asb[:, tt, 0:512], start=(tt == 0), stop=False), [b_tc[i2], b_A], [b_pb[6]], inc=False)
                        kb.op("pe", lambda tt=tt, i2=i2: E["pe"].matmul(pbank[6][:, :], ts_[i2][:, tt, :], Asb[:, tt, 512:1024], start=False, stop=(tt == 31)), [b_ts[i2], b_A], [b_pb[6]], inc=(tt == 31))
                        if tt % 4 == 3 and tt != 31:
                            yield
                    kb.op("act", lambda i2=i2: E["act"].activation(out=yst[i2][:], in_=pbank[6][:, :], func=AF.Copy), [b_pb[6]], [b_yst[i2]])
                    yield
                    for blk in range(4):
                        kb.op("pe", lambda blk=blk, i2=i2: E["pe"].transpose(pbank_b[0][:, blk * 128:(blk + 1) * 128], yst[i2][:, blk * 128:(blk + 1) * 128], identb), [b_yst[i2], b_cstb], [b_pbb[0]], inc=(blk == 3))
                    kb.op("dve", lambda i2=i2: E["dve"].tensor_copy(yst2[i2][:], pbank_b[0][:, 0:512].rearrange("p (b t) -> p b t", b=4)), [b_pbb[0]], [b_yst2[i2]])
                    kb.dma("sp", yT_d[kch, :, pas * 4:pas * 4 + 4, :], yst2[i2][:], [b_yst2[i2]], [], "ystb%d" % i2)
                    yield

        genF = gen_F()
        vpi = [0]; yti = [0]

        def prefetch_B(hd, ti):
            hl = hAld[ti % 2]; bhl = b_hAld[ti % 2]
            kb.dma("sp", hl[:], hA_d[ti * 128:(ti + 1) * 128, hd * 256:(hd + 1) * 256], [b_hAd[(hd, ti)]], [bhl], "hAld%d" % (ti % 2))
            o_ = ot[ti % 2]; bo = b_ot[ti % 2]
            kb.dma("sp", o_[:], otm_d[ti * 128:(ti + 1) * 128, hd * 256:(hd + 1) * 256], [], [bo], "ot%d" % (ti % 2))

        for hd in range(4):
            kb.dma("sp", qTh[:], qT_d[hd * 2:hd * 2 + 2].rearrange("c p t -> p c t"), [], [b_q], "hd_q")
            kb.dma("sp", kTh[:], kT_d[hd * 2:hd * 2 + 2].rearrange("c p t -> p c t"), [], [b_k], "hd_k")
            kb.dma("sp", ktmh[:], ktm_d[:, hd * 256:(hd + 1) * 256].rearrange("(i p) c -> p i c", p=128), [], [b_kt], "hd_kt")
            kb.dma("sp", vtmh[:], vtm_d[:, hd * 256:(hd + 1) * 256].rearrange("(i p) c -> p i c", p=128), [], [b_v], "hd_v")
            for d in range(2):
                kb.op("dve", lambda d=d: E["dve"].memset(Sf[:, d], 0.0), [], [b_Sf[d]])
                kb.op("dve", lambda d=d: E["dve"].memset(Sb[:, d, 0], 0.0), [], [b_Sb[d][0]])
            schedA = [(32, False), (33, False)] + [(c, True) for c in range(16)]
            schedB = [(33, False), (32, False)] + [(c, False) for c in range(31, 15, -1)] + [(c, True) for c in range(15, -1, -1)]
            steps = []
            for ib in range(18):
                steps.append((1, ib, schedB[ib])); steps.append((0, ib, schedA[ib]))
            for ib in range(18, 34):
                steps.append((1, ib, schedB[ib]))
            for (d, idx, (ti, full)) in steps:
                par = idx % 2
                col = d * 4 + hd
                wcol = wS[:, ti, col:col + 1]; rcol = rhoS[:, ti, col:col + 1]; dcol = decS[:, ti, col:col + 1]
                if d == 1 and full:
                    if ti == 15: prefetch_B(hd, 15)
                    if ti > 0: prefetch_B(hd, ti - 1)
                vi = vpi[0] % 4; vpi[0] += 1
                V = Vp[vi]; bV = b_Vp[vi]
                kb.op("act", lambda V=V, ti=ti, wcol=wcol: E["act"].activation(out=V[:, 0:256], in_=vtmh[:, ti, :], func=AF.Copy, scale=wcol), [b_v, b_wS], [bV])
                kb.op("dve", lambda V=V, wcol=wcol: E["dve"].tensor_copy(V[:, 256:257], wcol), [b_wS], [bV])
                pq = d * 3
                for dc in range(2):
                    kb.op("pe", lambda dc=dc, pq=pq, ti=ti, V=V: E["pe"].matmul(pbank[pq + 1 + dc][:, 0:257], ktmh[:, ti, dc * 128:(dc + 1) * 128], V[:], start=True, stop=True), [b_kt, bV], [b_pb[pq + 1 + dc]])
                    kb.op("dve", lambda dc=dc, pq=pq, d=d: E["dve"].tensor_tensor(tmpS[:, d, dc, :], pbank[pq + 1 + dc][:, 0:257], Sf[:, d, dc, :], ALU.add), [b_pb[pq + 1 + dc], b_Sf[d]], [b_tmpS[d]])
                kb.op("act", lambda d=d, dcol=dcol, par=par: E["act"].activation(out=Sb[:, d, 1 - par], in_=tmpS[:, d], func=AF.Copy, scale=dcol), [b_tmpS[d], b_decS], [b_Sb[d][1 - par]])
                kb.op("act", lambda d=d, dcol=dcol: E["act"].activation(out=Sf[:, d], in_=tmpS[:, d], func=AF.Copy, scale=dcol), [b_tmpS[d], b_decS], [b_Sf[d]])
                next(genF, None)
                if full:
                    tcs = slice(ti * 128, (ti + 1) * 128)
                    for dc in range(2):
                        kb.op("pe", lambda dc=dc, tcs=tcs, pq=pq: E["pe"].matmul(pbank[pq][:, 0:128], kTh[:, dc, tcs], qTh[:, dc, tcs], start=(dc == 0), stop=(dc == 1)),
                              [b_q, b_k], [b_pb[pq]], inc=(dc == 1))
                    msk = maskA if d == 0 else maskB
                    kb.op("dve", lambda d=d, pq=pq, msk=msk: E["dve"].tensor_tensor(STm[d][:], pbank[pq][:, 0:128], msk, ALU.mult), [b_pb[pq], b_cstb], [b_STm[d]])
                    for dc in range(2):
                        kb.op("pe", lambda dc=dc, tcs=tcs, pq=pq, d=d, par=par: E["pe"].matmul(pbank[pq][:, 128:385], qTh[:, dc, tcs], Sb[:, d, par, dc, :], start=(dc == 0), stop=False),
                              [b_q, b_Sb[d][par]], [b_pb[pq]], inc=False)
                    kb.op("pe", lambda pq=pq, d=d, V=V: E["pe"].matmul(pbank[pq][:, 128:385], STm[d][:], V[:], start=False, stop=True), [b_STm[d], bV], [b_pb[pq]])
                    s_ = sm[d]; bs = b_sm[d]
                    kb.op("act", lambda pq=pq, s_=s_: E["act"].activation(out=s_[:, 5:6], in_=pbank[pq][:, 384:385], func=AF.Abs), [b_pb[pq]], [bs])
                    kb.op("dve", lambda s_=s_, rcol=rcol: E["dve"].tensor_tensor(s_[:, 0:1], s_[:, 5:6], rcol, ALU.max), [bs, b_rhoS], [bs])
                    kb.op("dve", lambda s_=s_: E["dve"].reciprocal(s_[:, 1:2], s_[:, 0:1]), [bs], [bs])
                    if d == 0:
                        ha = hAst[ti % 2]; bha = b_hAst[ti % 2]
                        kb.op("act", lambda pq=pq, ha=ha, s_=s_: E["act"].activation(out=ha[:], in_=pbank[pq][:, 128:384], func=AF.Copy, scale=s_[:, 1:2]), [b_pb[pq], bs], [bha])
                        kb.dma("sp", hA_d[ti * 128:(ti + 1) * 128, hd * 256:(hd + 1) * 256], ha[:], [bha], [b_hAd.setdefault((hd, ti), Buf("hAd"))], "hAst%d" % (ti % 2))
                    else:
                        hl = hAld[ti % 2]; bhl = b_hAld[ti % 2]
                        o_ = ot[ti % 2]; bo = b_ot[ti % 2]
                        kb.op("dve", lambda pq=pq, s_=s_, hl=hl: E["dve"].scalar_tensor_tensor(hh[:], pbank[pq][:, 128:384], s_[:, 1:2], hl[:], ALU.mult, ALU.add), [b_pb[pq], bs, bhl], [b_hh])
                        kb.op("act", lambda s_=s_: E["act"].activation(out=junk2[:], in_=hh[:], func=AF.Square, accum_out=s_[:, 2:3]), [b_hh], [b_junk2, bs])
                        kb.op("act", lambda s_=s_: E["act"].activation(out=s_[:, 3:4], in_=s_[:, 2:3], func=AF.Ln, scale=1.0 / 256, bias=EPS), [bs], [bs])
                        kb.op("act", lambda s_=s_: E["act"].activation(out=s_[:, 4:5], in_=s_[:, 3:4], func=AF.Exp, scale=-0.5), [bs], [bs])
                        kb.op("act", lambda o_=o_: E["act"].activation(out=sg[:], in_=o_[:], func=AF.Exp, scale=-1.0), [bo], [b_sg])
                        kb.op("dve", lambda: E["dve"].tensor_scalar(sg[:], sg[:], 1.0, None, ALU.add), [b_sg], [b_sg])
                        kb.op("dve", lambda: E["dve"].reciprocal(sg[:], sg[:]), [b_sg], [b_sg])
                        kb.op("dve", lambda hd=hd: E["dve"].tensor_tensor(sg[:], sg[:], hgb[:, hd * 256:(hd + 1) * 256], ALU.mult), [b_sg, b_hgb], [b_sg])
                        kb.op("dve", lambda s_=s_: E["dve"].scalar_tensor_tensor(ym[:], hh[:], s_[:, 4:5], sg[:], ALU.mult, ALU.mult), [b_hh, bs, b_sg], [b_ym])
                        for blk in range(2):
                            kb.op("pe", lambda blk=blk: E["pe"].transpose(pbank_b[0][:, blk * 128:(blk + 1) * 128], ym[:, blk * 128:(blk + 1) * 128], identb), [b_ym, b_cstb], [b_pbb[0]], inc=(blk == 1))
                        yi = yti[0] % 2; yti[0] += 1
                        kb.op("act", lambda yi=yi: E["act"].activation(out=yts[yi][:], in_=pbank_b[0][:, 0:256].rearrange("p (b t) -> p b t", b=2), func=AF.Copy), [b_pbb[0]], [b_yts[yi]])
                        kb.dma("sp", yT_d[ti, :, 8 + hd * 2:10 + hd * 2, :], yts[yi][:], [b_yts[yi]], [], "yts%d" % yi)
                    next(genF, None)
        for _ in genF:
            pass
        kb.barrier()
    ph2.close()
    ph3 = ExitStack()
    pbank = [ps(ph3, "rb%d" % i, [128, 512], F32) for i in range(6)]
    pbank_b = [ps(ph3, "rbb%d" % i, [128, 1024], BF16) for i in range(2)]
    b_pb = [Buf("rb%d" % i) for i in range(6)]; b_pbb = [Buf("rbb0"), Buf("rbb1")]

    wout_v = wout_d.rearrange("(k p) n -> p k n", p=128)
    with ExitStack() as sg1:
        wo = sb(sg1, "wo", [128, 16, D], BF16); b_wo = Buf("wo")
        gt1 = sb(sg1, "gt1", [128, D], F32); b_gt1 = Buf("gt1")
        xt = [sb(sg1, "gxt%d" % i, [128, D], F32) for i in range(2)]; b_xt = [Buf("gxt0"), Buf("gxt1")]
        pt = [sb(sg1, "gpt%d" % i, [128, D], F32) for i in range(2)]; b_pt = [Buf("gpt0"), Buf("gpt1")]
        xn = [sb(sg1, "gxn%d" % i, [128, D], BF16) for i in range(2)]; b_xn = [Buf("gxn0"), Buf("gxn1")]
        yTt = [sb(sg1, "yTt%d" % i, [128, 16, 128], BF16) for i in range(2)]; b_yTt = [Buf("yTt0"), Buf("yTt1")]
        st = sb(sg1, "gst", [128, 2, 4], F32); b_st = [Buf("gst0"), Buf("gst1")]
        h2s = [sb(sg1, "h2s%d" % i, [128, 16, 128], BF16) for i in range(2)]; b_h2s = [Buf("h2s0"), Buf("h2s1")]
        kb.dma("sp", gt1[:], gt_d[0], [b_gtd[0]], [b_gt1], "gt1")
        for q4 in range(4):
            kb.dma("pool", wo[:, :, q4 * 512:(q4 + 1) * 512], wout_v[:, :, q4 * 512:(q4 + 1) * 512], [], [b_wo], "wo")
        for k in range(16):
            kb.op("dve", lambda k=k: E["dve"].tensor_tensor(wo[:, k, :], wo[:, k, :], gt1[:], ALU.mult), [b_wo, b_gt1], [b_wo])

        def g1_front(ti):
            i2 = ti % 2
            xb = xt[i2]; bxb = b_xt[i2]; pb_ = pt[i2]; bpb = b_pt[i2]; yt_ = yTt[i2]; byt = b_yTt[i2]
            kb.dma("sp", yt_[:], yT_d[ti], [], [byt], "yTt%d" % i2)
            kb.dma("sp", xb[:], x_d[ti * 128:(ti + 1) * 128, :], [], [bxb], "gxt%d" % i2)
            kb.dma("sp", pb_[:], pos_d[ti * 128:(ti + 1) * 128, :], [], [bpb], "gpt%d" % i2)
            kb.op("dve", lambda: E["dve"].tensor_tensor(xb[:], xb[:], pb_[:], ALU.add), [bxb, bpb], [bxb])
            for fc in range(4):
                pbn = 2 + fc
                for k in range(16):
                    kb.op("pe", lambda k=k, fc=fc, pbn=pbn: E["pe"].matmul(pbank[pbn][:, :], yt_[:, k, :], wo[:, k, fc * 512:(fc + 1) * 512], start=(k == 0), stop=(k == 15)),
                          [byt, b_wo], [b_pb[pbn]], inc=(k == 15))
                kb.op("dve", lambda fc=fc, pbn=pbn: E["dve"].tensor_tensor(xb[:, fc * 512:(fc + 1) * 512], pbank[pbn][:, :], xb[:, fc * 512:(fc + 1) * 512], ALU.add), [b_pb[pbn], bxb], [bxb])
            kb.dma("sp", x1_d[ti * 128:(ti + 1) * 128, :], xb[:], [bxb], [], "gxt%d" % i2)
            if debug:
                kb.dma("sp", dbg["x1"][ti * 128:(ti + 1) * 128, :], xb[:], [bxb], [], "gxt%d" % i2)
            kb.op("act", lambda: E["act"].activation(out=xn[i2][:], in_=xb[:], func=AF.Square, accum_out=st[:, i2, 0:1]), [bxb], [b_xn[i2], b_st[i2]])
            kb.op("act", lambda: E["act"].activation(out=st[:, i2, 1:2], in_=st[:, i2, 0:1], func=AF.Ln, scale=1.0 / D, bias=EPS), [b_st[i2]], [b_st[i2]])
            kb.op("act", lambda: E["act"].activation(out=st[:, i2, 2:3], in_=st[:, i2, 1:2], func=AF.Exp, scale=-0.5), [b_st[i2]], [b_st[i2]])
            kb.op("act", lambda: E["act"].activation(out=xn[i2][:], in_=xb[:], func=AF.Copy, scale=st[:, i2, 2:3]), [bxb, b_st[i2]], [b_xn[i2]])

        def g1_back(ti):
            i2 = ti % 2
            h2 = h2s[i2]; bh2 = b_h2s[i2]
            for hb in range(2):
                for j in range(8):
                    jj = hb * 8 + j
                    kb.op("pe", lambda jj=jj, j=j, hb=hb: E["pe"].transpose(pbank_b[hb][:, j * 128:(j + 1) * 128], xn[i2][:, jj * 128:(jj + 1) * 128], identb), [b_xn[i2], b_cstb], [b_pbb[hb]], inc=(j == 7))
                for j in range(8):
                    jj = hb * 8 + j
                    kb.op("dve", lambda jj=jj, j=j, hb=hb: E["dve"].tensor_scalar(h2[:, jj, :], pbank_b[hb][:, j * 128:(j + 1) * 128], g2[:, jj:jj + 1], modfm[:, 48 + jj, 0:1], ALU.mult, ALU.add),
                          [b_pbb[hb], b_g2, b_modfm], [bh2])
            kb.dma("sp", h2T_d[ti // 4, :, :, (ti % 4) * 128:(ti % 4 + 1) * 128], h2[:], [bh2], [], "h2s%d" % i2)

        for ti in range(OWN + 1):
            if ti < OWN: g1_front(ti)
            if ti >= 1: g1_back(ti - 1)
        kb.barrier()

    w1_v = w1b_d.rearrange("(k p) n -> p k n", p=128)
    w2_v = w2b_d.rearrange("(k p) n -> p k n", p=128)
    with ExitStack() as sg2:
        h2b = sb(sg2, "h2b", [128, 16, 512], BF16); b_h2b = Buf("h2b")
        hid = sb(sg2, "hid", [128, 64, 512], BF16); b_hid = [Buf("hid%d" % i) for i in range(16)]
        w1p = [sb(sg2, "w1p%d" % i, [128, 16, 512], BF16) for i in range(2)]; b_w1p = [Buf("w1p0"), Buf("w1p1")]
        w2p = [sb(sg2, "w2p%d" % i, [128, 8, 512], BF16) for i in range(3)]; b_w2p = [Buf("w2p%d" % i) for i in range(3)]
        x2 = [sb(sg2, "x2_%d" % i, [128, D], F32) for i in range(4)]; b_x2 = [Buf("x2_%d" % i) for i in range(4)]
        gf = sb(sg2, "gf", [128, D], F32); b_gf = Buf("gf")
        gt2 = sb(sg2, "gt2", [128, D], F32); b_gt2 = Buf("gt2")
        rl = [sb(sg2, "rl%d" % i, [128, 512], F32) for i in range(2)]; b_rl = [Buf("rl0"), Buf("rl1")]
        st = sb(sg2, "fst", [128, 4], F32); b_st = Buf("fst")
        kb.dma("sp", gf[:], gfin_d, [], [b_gf], "gf")
        kb.dma("sp", gt2[:], gt_d[1], [b_gtd[1]], [b_gt2], "gt2")
        w1i = 0; w2i = 0; rli = 0
        for tb in range(4):
            kb.dma("sp", h2b[:], h2T_d[tb], [], [b_h2b], "h2b")
            for i in range(4):
                ti = tb * 4 + i
                kb.dma("sp", x2[i][:], x1_d[ti * 128:(ti + 1) * 128, :], [], [b_x2[i]], "x2_%d" % i)
            for pj in range(16):
                w = w1p[w1i % 2]; bw = b_w1p[w1i % 2]
                kb.dma("sp", w[:], w1_v[:, :, pj * 512:(pj + 1) * 512], [b_w1b], [bw], "w1p%d" % (w1i % 2)); w1i += 1
                for jc in range(4):
                    pbn = rli % 2
                    for k in range(16):
                        kb.op("pe", lambda k=k, jc=jc, w=w, pbn=pbn: E["pe"].matmul(pbank[pbn][:, :], w[:, k, jc * 128:(jc + 1) * 128], h2b[:, k, :], start=(k == 0), stop=(k == 15)), [bw, b_h2b], [b_pb[pbn]], inc=(k == 15))
                    r_ = rl[rli % 2]; br = b_rl[rli % 2]; rli += 1
                    kb.op("act", lambda pbn=pbn, r_=r_: E["act"].activation(out=r_[:], in_=pbank[pbn][:, :], func=AF.Relu), [b_pb[pbn]], [br])
                    kb.op("dve", lambda r_=r_, pj=pj, jc=jc: E["dve"].tensor_tensor(hid[:, pj * 4 + jc, :], r_[:], r_[:], ALU.mult), [br], [b_hid[pj]])
            for fc in range(4):
                for jq in range(8):
                    w = w2p[w2i % 3]; bw = b_w2p[w2i % 3]
                    kb.dma("sp", w[:], w2_v[:, jq * 8:(jq + 1) * 8, fc * 512:(fc + 1) * 512], [b_w2b], [bw], "w2p%d" % (w2i % 3)); w2i += 1
                    for i in range(4):
                        pbn = 2 + i
                        for k in range(8):
                            jc = jq * 8 + k
                            kb.op("pe", lambda k=k, jc=jc, i=i, w=w, pbn=pbn, jq=jq: E["pe"].matmul(pbank[pbn][:, :], hid[:, jc, i * 128:(i + 1) * 128], w[:, k, :], start=(jq == 0 and k == 0), stop=(jq == 7 and k == 7)),
                                  [bw, b_hid[jc // 4]], [b_pb[pbn]], inc=(k == 7))
                for i in range(4):
                    pbn = 2 + i
                    r_ = rl[rli % 2]; br = b_rl[rli % 2]; rli += 1
                    kb.op("dve", lambda fc=fc, pbn=pbn, r_=r_: E["dve"].tensor_tensor(r_[:], pbank[pbn][:, :], gt2[:, fc * 512:(fc + 1) * 512], ALU.mult), [b_pb[pbn], b_gt2], [br])
                    kb.op("dve", lambda fc=fc, i=i, r_=r_: E["dve"].tensor_tensor(x2[i][:, fc * 512:(fc + 1) * 512], r_[:], x2[i][:, fc * 512:(fc + 1) * 512], ALU.add), [br, b_x2[i]], [b_x2[i]])
            for i in range(4):
                ti = tb * 4 + i
                kb.op("act", lambda i=i: E["act"].activation(out=hid[:, 0:4, :].rearrange("p a b -> p (a b)"), in_=x2[i][:], func=AF.Square, accum_out=st[:, 0:1]), [b_x2[i]], [b_hid[0], b_st])
                kb.op("act", lambda: E["act"].activation(out=st[:, 1:2], in_=st[:, 0:1], func=AF.Ln, scale=1.0 / D, bias=EPS), [b_st], [b_st])
                kb.op("act", lambda: E["act"].activation(out=st[:, 2:3], in_=st[:, 1:2], func=AF.Exp, scale=-0.5), [b_st], [b_st])
                kb.op("dve", lambda i=i: E["dve"].scalar_tensor_tensor(x2[i][:], x2[i][:], st[:, 2:3], gf[:], ALU.mult, ALU.mult), [b_x2[i], b_st, b_gf], [b_x2[i]])
                kb.dma("sp", out_d[ti * 128:(ti + 1) * 128, :], x2[i][:], [b_x2[i]], [], "x2_%d" % i)
        kb.final_wait()
    ph3.close()
    es.close()
    return nc


def _pos_table():
    quarter = D // 4
    omega = (1.0 / (np.float32(10000.0) ** (np.arange(quarter, dtype=np.float32) / np.float32(quarter)))).astype(np.float32)
    ar = (np.arange(64, dtype=np.float32)[:, None] * omega).astype(np.float32)
    emb = np.concatenate([np.sin(ar), np.cos(ar)], -1).astype(np.float32)
    row = np.broadcast_to(emb[:, None, :], (64, 64, 1024))
    col = np.broadcast_to(emb[None, :, :], (64, 64, 1024))
    return np.ascontiguousarray(np.concatenate([row, col], -1).reshape(T, D).astype(np.float32))


def _consts():
    c = {}
    c["pos"] = _pos_table()
    j = np.arange(256)[:, None].astype(np.int64); m = np.arange(256)[None, :].astype(np.int64)
    ang = 2.0 * np.pi * ((j * m) % 256) / 256.0
    cs = np.concatenate([np.cos(ang), np.sin(ang)], axis=1)
    c["csd"] = np.ascontiguousarray(cs.reshape(2, 128, 512).transpose(1, 0, 2)).astype(NPBF)
    s = np.arange(128)[:, None]; t = np.arange(128)[None, :]
    cst = np.zeros((128, 5, 128), np.float32)
    cst[:, 0, :] = (s <= t); cst[:, 1, :] = (s >= t); cst[:, 2, :] = (s == t); cst[:, 3, :] = 1.0
    c["cst"] = cst
    for fl in range(2):
        tp = np.arange(T, dtype=np.int64); ii = np.arange(2048, dtype=np.int64)
        to = (T - 1 - tp) if fl else tp
        ko = (T - 1 - ii) if fl else ii
        ang = 2.0 * np.pi * ((to[:, None] * ko[None, :]) % T) / T
        for nm, fn, sgn in (("tabc", np.cos, 1.0), ("tabs", np.sin, -1.0)):
            tab = (sgn * fn(ang) / 1024.0).astype(np.float32)
            tab = tab.reshape(32, 128, 16, 128).transpose(2, 1, 0, 3)
            c["%s%d" % (nm, fl)] = np.ascontiguousarray(tab).astype(NPBF)
    return c


def fm(v, n):
    return np.ascontiguousarray(v.reshape(n, 128).T.astype(np.float32))


def bc(v):
    return np.ascontiguousarray(np.broadcast_to(v.reshape(1, -1), (128, v.size)).astype(np.float32))


_CACHE = {}


def kernel(x, c, ctx, c_ctx, w_mod, b_mod, g_mix, g_mlp, w_in, conv_w, gate_b, head_g, w_out, w_mlp1, w_mlp2, g_final, _debug=False):
    x = np.asarray(x, np.float32); ctx = np.asarray(ctx, np.float32)
    if "c" not in _CACHE: _CACHE["c"] = _consts()
    C = _CACHE["c"]
    w_in0 = np.asarray(w_in[0], np.float32)
    perm = np.concatenate([np.arange(5120), 5120 + 8 + np.arange(8), 5120 + np.arange(8)])
    w_in1 = np.ascontiguousarray(w_in0[:, perm])
    gb0 = np.asarray(gate_b[0], np.float32).reshape(16); gb1 = np.asarray(gate_b[0], np.float32)[::-1].reshape(16)
    cw0 = np.asarray(conv_w[0], np.float32); cw1 = cw0[::-1]
    bm = np.asarray(b_mod[0], np.float32)
    shared = {
        "w_mod": np.ascontiguousarray(np.asarray(w_mod[0], np.float32)), "bmod_fm": fm(bm, 96),
        "bmod_bc": np.ascontiguousarray(np.stack([bc(bm[4096:6144]), bc(bm[10240:12288])], axis=1)),
        "gmix_fm": fm(np.asarray(g_mix[0], np.float32), 16), "gmlp_fm": fm(np.asarray(g_mlp[0], np.float32), 16),
        "headg_bc": bc(np.asarray(head_g[0], np.float32)), "gfin_bc": bc(np.asarray(g_final, np.float32)),
        "w_out": np.ascontiguousarray(np.asarray(w_out[0], np.float32)), "w_mlp1": np.ascontiguousarray(np.asarray(w_mlp1[0], np.float32)),
        "w_mlp2": np.ascontiguousarray(np.asarray(w_mlp2[0], np.float32)), "csd": C["csd"], "cst": C["cst"],
    }
    in_maps = []
    for core in range(8):
        b = core // 2; fl = core % 2
        xb = x[b][::-1] if fl else x[b]; cb = ctx[b][::-1] if fl else ctx[b]
        cc = np.stack([np.asarray(c[b], np.float32), np.asarray(c_ctx, np.float32)], axis=0)
        ccT = np.ascontiguousarray(cc.reshape(2, 16, 128).transpose(2, 1, 0))
        cw = cw1 if fl else cw0
        m = dict(shared)
        m.update({
            "x": np.ascontiguousarray(xb), "ctx": np.ascontiguousarray(cb), "pos": np.ascontiguousarray(C["pos"][::-1]) if fl else C["pos"],
            "ccT": ccT, "w_in": w_in1 if fl else w_in0,
            "conv_fm": np.ascontiguousarray(cw.reshape(5, 16, 128).transpose(2, 1, 0)), "gateb_bc": bc(gb1 if fl else gb0),
            "tabc": C["tabc%d" % fl], "tabs": C["tabs%d" % fl],
        })
        in_maps.append(m)
    key = "nc_dbg" if _debug else "nc"
    if key not in _CACHE: _CACHE[key] = build_program(debug=_debug)
    res = run_bass_kernel_spmd(_CACHE[key], in_maps, core_ids=list(range(8)))
    out = np.empty((4, T, D), np.float32)
    for core in range(8):
        b = core // 2; fl = core % 2
        o = np.asarray(res.results[core]["out"], np.float32)
        if fl: out[b, 2048:] = o[::-1]
        else: out[b, :2048] = o
    if _debug:
        return out, res
    return out
```

```python
import numpy as np
import ml_dtypes
from contextlib import ExitStack
import concourse.bass as bass
import concourse.mybir as mybir
from concourse.bass_utils import run_bass_kernel_spmd

F32 = mybir.dt.float32
BF16 = mybir.dt.bfloat16
AF = mybir.ActivationFunctionType
ALU = mybir.AluOpType
NPBF = ml_dtypes.bfloat16

D = 2048; T = 4096; TC = 256; NTS = 32; NTT = 34; OWN = 16
EPS = 1e-6


class Buf:
    __slots__ = ("name", "w", "r")

    def __init__(self, name):
        self.name = name; self.w = None; self.r = {}


class KB:
    def __init__(self, nc, es):
        self.nc = nc; self.es = es
        self.E = {"pe": nc.tensor, "act": nc.scalar, "dve": nc.vector, "pool": nc.gpsimd, "sp": nc.sync}
        self.sem = {k: es.enter_context(nc.semaphore("s_" + k)) for k in ["pe", "act", "dve", "pool"]}
        self.cnt = {k: 0 for k in self.sem}
        self.waited = {k: {} for k in self.E}
        self.dsems = {}
        self.bar = es.enter_context(nc.semaphore("s_bar")); self.barn = 0
        self.nobar = {"wcast"}

    def _wait(self, eng, toks):
        for (key, sem, val, src, kind) in toks:
            if src == eng and kind != "raw":
                continue
            if self.waited[eng].get(key, 0) >= val:
                continue
            self.E[eng].wait_ge(sem, val); self.waited[eng][key] = val

    def _deps(self, reads, writes):
        toks = []
        for b in reads:
            if b.w: toks.append(b.w + ("raw",))
        for b in writes:
            if b.w: toks.append(b.w + ("waw",))
            for t in b.r.values(): toks.append(t + ("war",))
        return toks

    def _post(self, tok, reads, writes):
        for b in reads:
            o = b.r.get(tok[0])
            if o is None or o[2] < tok[2]: b.r[tok[0]] = tok
        for b in writes:
            b.w = tok; b.r = {}

    def op(self, eng, fn, reads=(), writes=(), inc=True):
        self._wait(eng, self._deps(reads, writes))
        ins = fn()
        if inc:
            self.cnt[eng] += 1; ins.then_inc(self.sem[eng], 1)
            tok = (eng, self.sem[eng], self.cnt[eng], eng)
        else:
            tok = (eng, self.sem[eng], self.cnt[eng] + 1, eng)
        self._post(tok, reads, writes)
        return ins

    def dma(self, q, out, in_, reads, writes, semname):
        self._wait(q, self._deps(reads, writes))
        if semname not in self.dsems:
            self.dsems[semname] = [self.es.enter_context(self.nc.semaphore("d_" + semname)), 0]
        ent = self.dsems[semname]
        ins = self.E[q].dma_start(out=out, in_=in_)
        ent[1] += 16; ins.then_inc(ent[0], 16)
        tok = ("d_" + semname, ent[0], ent[1], None)
        self._post(tok, reads, writes)

    def barrier(self):
        sp = self.E["sp"]
        for k in self.sem:
            if self.cnt[k] > self.waited["sp"].get(k, 0):
                sp.wait_ge(self.sem[k], self.cnt[k]); self.waited["sp"][k] = self.cnt[k]
        for name, (sem, val) in self.dsems.items():
            if name in self.nobar: continue
            if val > self.waited["sp"].get("d_" + name, 0):
                sp.wait_ge(sem, val); self.waited["sp"]["d_" + name] = val
        self.barn += 1
        sp.sem_inc(self.bar, 1)
        for k in ["pe", "act", "dve", "pool"]:
            self.E[k].wait_ge(self.bar, self.barn)
            for k2 in self.sem: self.waited[k][k2] = max(self.waited[k].get(k2, 0), self.cnt[k2])
            for name, (sem, val) in self.dsems.items():
                if name in self.nobar: continue
                self.waited[k]["d_" + name] = val

    def final_wait(self):
        sp = self.E["sp"]
        for name, (sem, val) in self.dsems.items():
            if val > self.waited["sp"].get("d_" + name, 0):
                sp.wait_ge(sem, val)
        for k in self.sem:
            if self.cnt[k] > self.waited["sp"].get(k, 0):
                sp.wait_ge(self.sem[k], self.cnt[k])


def build_program(debug=False):
    nc = bass.Bass("TRN2", target_bir_lowering=False)
    es = ExitStack()
    kb = KB(nc, es)
    E = kb.E

    def din(name, shape, dt=F32):
        return nc.dram_tensor(name, list(shape), dt, kind="ExternalInput").ap()

    def dscr(name, shape, dt):
        return nc.dram_tensor(name, list(shape), dt, kind="Internal").ap()

    x_d = din("x", [T, D]); ctx_d = din("ctx", [TC, D]); pos_d = din("pos", [T, D])
    ccT_d = din("ccT", [128, 16, 2])
    wmod_d = din("w_mod", [D, 6 * D]); bmodfm_d = din("bmod_fm", [128, 96]); bmodbc_d = din("bmod_bc", [128, 2, D])
    gmix_d = din("gmix_fm", [128, 16]); gmlp_d = din("gmlp_fm", [128, 16])
    win_d = din("w_in", [D, 5136]); convfm_d = din("conv_fm", [128, 16, 5]); gateb_d = din("gateb_bc", [128, 16])
    headg_d = din("headg_bc", [128, 1024]); gfin_d = din("gfin_bc", [128, D])
    wout_d = din("w_out", [D, D]); w1_d = din("w_mlp1", [D, 4 * D]); w2_d = din("w_mlp2", [4 * D, D])
    csd_d = din("csd", [128, 2, 512], BF16)
    tabc_d = din("tabc", [16, 128, 32, 128], BF16); tabs_d = din("tabs", [16, 128, 32, 128], BF16)
    cst_d = din("cst", [128, 5, 128])
    out_d = nc.dram_tensor("out", [2048, D], F32, kind="ExternalOutput").ap()
    A_d = dscr("A_scr", [2, NTS, 128, 1024], BF16)
    qkpre_d = dscr("qkpre_scr", [16, 128, NTT * 128], BF16)
    qT_d = dscr("qT_scr", [8, 128, 2048], BF16)
    kT_d = dscr("kT_scr", [8, 128, NTT * 128], BF16)
    ktm_d = dscr("ktm_scr", [NTT * 128, 1024], BF16)
    vtm_d = dscr("vtm_scr", [NTT * 128, 1024], BF16)
    otm_d = dscr("otm_scr", [2048, 1024], BF16)
    hA_d = dscr("hA_scr", [2048, 1024], F32)
    x1_d = dscr("x1_scr", [2048, D], F32)
    yT_d = dscr("yT_scr", [OWN, 128, 16, 128], BF16)
    gt_d = dscr("gt_scr", [2, 128, D], F32)
    w1b_d = dscr("w1b_scr", [D, 4 * D], BF16)
    w2b_d = dscr("w2b_scr", [4 * D, D], BF16)
    h2T_d = dscr("h2T_scr", [4, 128, 16, 512], BF16)
    dbg = {}
    if debug:
        dbg["x1"] = nc.dram_tensor("dbg_x1", [2048, D], F32, kind="ExternalOutput").ap()
        dbg["mod"] = nc.dram_tensor("dbg_mod", [128, 96, 2], F32, kind="ExternalOutput").ap()
        dbg["gates"] = nc.dram_tensor("dbg_gates", [128, NTT, 16], F32, kind="ExternalOutput").ap()

    def sb(stack, name, shape, dt):
        return stack.enter_context(nc.sbuf_tensor("sb_" + name, list(shape), dt))

    def ps(stack, name, shape, dt):
        return stack.enter_context(nc.psum_tensor("ps_" + name, list(shape), dt))

    cst = sb(es, "cst", [128, 5, 128], F32); b_cst = Buf("cst")
    cstb = sb(es, "cstb", [128, 5, 128], BF16); b_cstb = Buf("cstb")
    modfm = sb(es, "modfm", [128, 96, 2], F32); b_modfm = Buf("modfm")
    g1 = sb(es, "g1", [128, 16, 2], F32); b_g1 = Buf("g1")
    g2 = sb(es, "g2", [128, 16], F32); b_g2 = Buf("g2")
    b_gtd = [Buf("gtd0"), Buf("gtd1")]
    b_w1b = Buf("w1b"); b_w2b = Buf("w2b")
    gtm = sb(es, "gtm", [128, NTT, 16], F32); b_gtm = Buf("gtm")
    wS = sb(es, "wS", [128, NTT, 8], F32); rhoS = sb(es, "rhoS", [128, NTT, 8], F32); decS = sb(es, "decS", [128, NTT, 8], F32)
    b_wS = Buf("wS"); b_rhoS = Buf("rhoS"); b_decS = Buf("decS")
    b_yT = [Buf("yT%d" % i) for i in range(OWN)]
    ph1 = ExitStack()
    pbank = [ps(ph1, "pb%d" % i, [128, 512], F32) for i in range(6)]
    pbank_b = [ps(ph1, "pbb%d" % i, [128, 1024], BF16) for i in range(2)]
    b_pb = [Buf("pb%d" % i) for i in range(6)]; b_pbb = [Buf("pbb0"), Buf("pbb1")]

    maskA = cstb[:, 0, :]; maskB = cstb[:, 1, :]; identb = cstb[:, 2, :]
    triA = cst[:, 0, :]; triB = cst[:, 1, :]; onesf = cst[:, 3, :]

    kb.dma("sp", cst[:], cst_d, [], [b_cst], "cst")
    kb.op("dve", lambda: E["dve"].tensor_copy(cstb[:], cst[:]), [b_cst], [b_cstb])

    wmod_v = wmod_d.rearrange("(k p) n -> p k n", p=128)

    def stage0(stack, pcs, first):
        ccT = sb(stack, "ccT%d" % first, [128, 16, 2], F32); b_ccT = Buf("ccT")
        scT = sb(stack, "scT%d" % first, [128, 16, 2], BF16); b_scT = Buf("scT")
        bmfm = sb(stack, "bmfm%d" % first, [128, 96], F32); b_bmfm = Buf("bmfm")
        gmx = sb(stack, "gmx%d" % first, [128, 16], F32); b_gm = Buf("gm")
        wp = [sb(stack, "wmp%d_%d" % (first, i), [128, 16, 512], BF16) for i in range(2)]; b_wp = [Buf("wmp0"), Buf("wmp1")]
        sfx = "_%d" % first
        kb.dma("sp", ccT[:], ccT_d, [], [b_ccT], "ccT" + sfx)
        kb.dma("sp", bmfm[:], bmodfm_d, [], [b_bmfm], "bmfm" + sfx)
        kb.dma("sp", gmx[:], gmix_d if first else gmlp_d, [], [b_gm], "gmx" + sfx)
        kb.op("act", lambda: E["act"].activation(out=scT[:], in_=ccT[:], func=AF.Silu), [b_ccT], [b_scT])
        if not first:
            scR = sb(stack, "scR", [128, 16, 128], BF16); b_scR = Buf("scR")
            bmbc = sb(stack, "bmbc", [128, 2, D], F32); b_bmbc = Buf("bmbc")
            gts = [sb(stack, "gts%d" % i, [128, 512], F32) for i in range(2)]; b_gts = [Buf("gts0"), Buf("gts1")]
            kb.dma("sp", bmbc[:], bmodbc_d, [], [b_bmbc], "bmbc")
            for k in range(16):
                kb.op("act", lambda k=k: E["act"].activation(out=scR[:, k, :], in_=ccT[:, k, 0:1].to_broadcast([128, 128]), func=AF.Silu), [b_ccT], [b_scR])
        gi_ = 0
        for n_, pc in enumerate(pcs):
            blk = pc // 4
            w = wp[n_ % 2]; bw = b_wp[n_ % 2]
            kb.dma("pool", w[:], wmod_v[:, :, pc * 512:(pc + 1) * 512], [], [bw], "wmp%d%s" % (n_ % 2, sfx))
            if blk in (0, 1, 3, 4):
                for fcn in range(4):
                    ch = pc * 4 + fcn
                    for k in range(16):
                        kb.op("pe", lambda k=k, fcn=fcn, w=w: E["pe"].matmul(pbank[0][:, 0:2], w[:, k, fcn * 128:(fcn + 1) * 128], scT[:, k, :], start=(k == 0), stop=(k == 15)),
                              [bw, b_scT], [b_pb[0]], inc=(k == 15))
                    kb.op("dve", lambda ch=ch: E["dve"].tensor_tensor(modfm[:, ch, :], pbank[0][:, 0:2], bmfm[:, ch:ch + 1].to_broadcast([128, 2]), ALU.add),
                          [b_pb[0], b_bmfm], [b_modfm])
            else:
                gi = 0 if blk == 2 else 1
                col = (pc % 4) * 512
                for k in range(16):
                    kb.op("pe", lambda k=k, w=w: E["pe"].matmul(pbank[1][:, :], scR[:, k, :], w[:, k, :], start=(k == 0), stop=(k == 15)),
                          [bw, b_scR], [b_pb[1]], inc=(k == 15))
                g_ = gts[gi_ % 2]; bg = b_gts[gi_ % 2]; gname = "gts%d" % (gi_ % 2); gi_ += 1
                kb.op("dve", lambda gi=gi, col=col, g_=g_: E["dve"].tensor_tensor(g_[:], pbank[1][:, :], bmbc[:, gi, col:col + 512], ALU.add),
                      [b_pb[1], b_bmbc], [bg])
                kb.dma("sp", gt_d[gi, :, col:col + 512], g_[:], [bg], [b_gtd[gi]], gname)
            yield
        if first:
            kb.op("dve", lambda: E["dve"].tensor_scalar(g1[:], modfm[:, 16:32, :], 1.0, None, ALU.add), [b_modfm], [b_g1])
            kb.op("dve", lambda: E["dve"].tensor_tensor(g1[:, :, 0], g1[:, :, 0], gmx[:], ALU.mult), [b_g1, b_gm], [b_g1])
            kb.op("dve", lambda: E["dve"].tensor_tensor(g1[:, :, 1], g1[:, :, 1], gmx[:], ALU.mult), [b_g1, b_gm], [b_g1])
        else:
            kb.op("dve", lambda: E["dve"].tensor_scalar(g2[:], modfm[:, 64:80, 0], 1.0, None, ALU.add), [b_modfm], [b_g2])
            kb.op("dve", lambda: E["dve"].tensor_tensor(g2[:], g2[:], gmx[:], ALU.mult), [b_g2, b_gm], [b_g2])
        yield

    with ExitStack() as s0:
        for _ in stage0(s0, list(range(8)), 1):
            pass
        kb.barrier()

    def load_tile_src(ti):
        if ti < NTS:
            return x_d[ti * 128:(ti + 1) * 128, :], pos_d[ti * 128:(ti + 1) * 128, :]
        return ctx_d[(ti - NTS) * 128:(ti - NTS + 1) * 128, :], None

    win_v = win_d.rearrange("(k p) n -> p k n", p=128)
    with ExitStack() as sab:
        hxT = sb(sab, "hxT", [128, 16, 18 * 128], BF16)
        b_hx = [Buf("hx%d" % i) for i in range(18)]
        xt = [sb(sab, "xt%d" % i, [128, D], F32) for i in range(2)]; b_xt = [Buf("xt0"), Buf("xt1")]
        pt = [sb(sab, "pt%d" % i, [128, D], F32) for i in range(2)]; b_pt = [Buf("pt0"), Buf("pt1")]
        xn = sb(sab, "xn", [128, D], BF16); b_xn = Buf("xn")
        junk = sb(sab, "junk", [128, D], BF16); b_junk = Buf("junk")
        st = sb(sab, "st", [128, 4], F32); b_st = Buf("st")
        wpc = [sb(sab, "wpc%d" % i, [128, 16, 512], BF16) for i in range(2)]; b_wpc = [Buf("wpc0"), Buf("wpc1")]
        wg = sb(sab, "wg", [128, 16, 16], BF16); b_wg = Buf("wg")
        csd = sb(sab, "csd", [128, 2, 512], BF16); b_csd = Buf("csd")
        uT = sb(sab, "uT", [128, 4, 512], BF16); b_uT = [Buf("uT%d" % i) for i in range(4)]
        ast = [sb(sab, "ast%d" % i, [128, 2, 2, 256], BF16) for i in range(4)]; b_ast = [Buf("ast%d" % i) for i in range(4)]
        stg = [sb(sab, "stg%d" % i, [128, 512], BF16) for i in range(4)]; b_stg = [Buf("stg%d" % i) for i in range(4)]
        kb.dma("sp", csd[:], csd_d, [], [b_csd], "csd")
        hgb0 = sb(sab, "hgb0", [128, 1024], F32); b_hgb0 = Buf("hgb0")
        sgf = sb(sab, "sgf", [128, 512], F32); b_sgf = Buf("sgf")
        kb.dma("sp", hgb0[:], headg_d, [], [b_hgb0], "hgb0")
        kb.dma("pool", wg[:], win_v[:, :, 5120:5136], [], [b_wg], "wg")
        stgi = [0]; pbi = [0]

        def next_stg():
            i = stgi[0] % 4; stgi[0] += 1; return i

        def next_pb():
            i = 2 + (pbi[0] % 4); pbi[0] += 1; return i

        for half in range(2):
            tiles = list(range(0, 16)) if half == 0 else list(range(16, 34))
            for si, ti in enumerate(tiles):
                xs, psrc = load_tile_src(ti)
                r = 0 if ti < NTS else 1
                xb = xt[si % 2]; bxb = b_xt[si % 2]
                kb.dma("sp", xb[:], xs, [], [bxb], "xt%d" % (si % 2))
                if psrc is not None:
                    pb_ = pt[si % 2]; bpb = b_pt[si % 2]
                    kb.dma("sp", pb_[:], psrc, [], [bpb], "pt%d" % (si % 2))
                    kb.op("dve", lambda xb=xb, pb_=pb_: E["dve"].tensor_tensor(xb[:], xb[:], pb_[:], ALU.add), [bxb, bpb], [bxb])
                kb.op("act", lambda xb=xb: E["act"].activation(out=junk[:], in_=xb[:], func=AF.Square, accum_out=st[:, 0:1]), [bxb], [b_junk, b_st])
                kb.op("act", lambda: E["act"].activation(out=st[:, 1:2], in_=st[:, 0:1], func=AF.Ln, scale=1.0 / D, bias=EPS), [b_st], [b_st])
                kb.op("act", lambda: E["act"].activation(out=st[:, 2:3], in_=st[:, 1:2], func=AF.Exp, scale=-0.5), [b_st], [b_st])
                kb.op("act", lambda xb=xb: E["act"].activation(out=xn[:], in_=xb[:], func=AF.Copy, scale=st[:, 2:3]), [bxb, b_st], [b_xn])
                for hb in range(2):
                    for j in range(8):
                        jj = hb * 8 + j
                        kb.op("pe", lambda jj=jj, j=j, hb=hb: E["pe"].transpose(pbank_b[hb][:, j * 128:(j + 1) * 128], xn[:, jj * 128:(jj + 1) * 128], identb),
                              [b_xn, b_cstb], [b_pbb[hb]], inc=(j == 7))
                    for j in range(8):
                        jj = hb * 8 + j
                        kb.op("dve", lambda jj=jj, j=j, hb=hb, si=si, r=r: E["dve"].tensor_scalar(
                            hxT[:, jj, si * 128:(si + 1) * 128], pbank_b[hb][:, j * 128:(j + 1) * 128], g1[:, jj, r:r + 1], modfm[:, jj, r:r + 1], ALU.mult, ALU.add),
                            [b_pbb[hb], b_g1, b_modfm], [b_hx[si]])
            nt = len(tiles)
            blocks = [(b0, min(4, nt - b0)) for b0 in range(0, nt, 4)]
            pieces = []
            pieces += [("f", 0, 0), ("f", 512, 1)]
            pieces += [("q", 1024, 0), ("q", 1536, 1)]
            pieces += [("k", 2048, 0), ("k", 2560, 1)]
            pieces += [("v", 3072, 0), ("v", 3584, 1)]
            if half == 0:
                pieces += [("o", 4096, 0), ("o", 4608, 1)]
            for pi, (kind, col0, idx) in enumerate(pieces):
                w = wpc[pi % 2]; bw = b_wpc[pi % 2]
                kb.dma("pool", w[:], win_v[:, :, col0:col0 + 512], [], [bw], "wpc%d" % (pi % 2))
                for (b0, nb) in blocks:
                    ncol = nb * 128
                    tiles_b = tiles[b0:b0 + nb]
                    is_ctx = tiles_b[0] >= NTS
                    hxbufs = [b_hx[b0 + i] for i in range(nb)]
                    if kind == "f":
                        if is_ctx: continue
                        for cc in range(4):
                            pbn = next_pb()
                            for k in range(16):
                                kb.op("pe", lambda k=k, cc=cc, w=w, pbn=pbn, b0=b0, ncol=ncol: E["pe"].matmul(pbank[pbn][:, 0:ncol], w[:, k, cc * 128:(cc + 1) * 128], hxT[:, k, b0 * 128:b0 * 128 + ncol], start=(k == 0), stop=(k == 15)),
                                      [bw] + hxbufs, [b_pb[pbn]], inc=(k == 15))
                            kb.op("act", lambda cc=cc, pbn=pbn, ncol=ncol: E["act"].activation(out=uT[:, cc, 0:ncol], in_=pbank[pbn][:, 0:ncol], func=AF.Copy), [b_pb[pbn]], [b_uT[cc]])
                        for i in range(nb):
                            ti = tiles_b[i]
                            ai = ti % 4
                            for g in range(2):
                                pbn = next_pb()
                                for jc in range(2):
                                    kb.op("pe", lambda g=g, jc=jc, i=i, pbn=pbn: E["pe"].matmul(pbank[pbn][:, :], uT[:, g * 2 + jc, i * 128:(i + 1) * 128], csd[:, jc, :], start=(jc == 0), stop=(jc == 1)),
                                          [b_uT[g * 2 + jc], b_csd], [b_pb[pbn]], inc=(jc == 1))
                                kb.op("dve", lambda g=g, pbn=pbn, ai=ai: E["dve"].tensor_copy(ast[ai][:, :, g, :], pbank[pbn][:, :].rearrange("p (c m) -> p c m", c=2)), [b_pb[pbn]], [b_ast[ai]])
                            kb.dma("sp", A_d[idx, ti], ast[ai][:].rearrange("p a b c -> p (a b c)"), [b_ast[ai]], [], "ast%d" % ai)
                    elif kind in ("q", "k"):
                        if kind == "q" and not (half == 0 or b0 == 0): continue
                        for cc in range(4):
                            pbn = next_pb()
                            for k in range(16):
                                kb.op("pe", lambda k=k, cc=cc, w=w, pbn=pbn, b0=b0, ncol=ncol: E["pe"].matmul(pbank[pbn][:, 0:ncol], w[:, k, cc * 128:(cc + 1) * 128], hxT[:, k, b0 * 128:b0 * 128 + ncol], start=(k == 0), stop=(k == 15)),
                                      [bw] + hxbufs, [b_pb[pbn]], inc=(k == 15))
                            si_ = next_stg()
                            kb.op("act", lambda pbn=pbn, ncol=ncol, si_=si_: E["act"].activation(out=stg[si_][:, 0:ncol], in_=pbank[pbn][:, 0:ncol], func=AF.Copy), [b_pb[pbn]], [b_stg[si_]])
                            chn = (0 if kind == "q" else 8) + idx * 4 + cc
                            t0 = tiles_b[0] * 128
                            kb.dma("sp", qkpre_d[chn, :, t0:t0 + ncol], stg[si_][:, 0:ncol], [b_stg[si_]], [], "stg%d" % si_)
                    else:
                        for i in range(nb):
                            ti = tiles_b[i]
                            pbn = next_pb()
                            for k in range(16):
                                kb.op("pe", lambda k=k, w=w, pbn=pbn, b0=b0, i=i: E["pe"].matmul(pbank[pbn][:, :], hxT[:, k, (b0 + i) * 128:(b0 + i + 1) * 128], w[:, k, :], start=(k == 0), stop=(k == 15)),
                                      [bw, b_hx[b0 + i]], [b_pb[pbn]], inc=(k == 15))
                            si_ = next_stg()
                            if kind == "v":
                                kb.op("act", lambda pbn=pbn, si_=si_: E["act"].activation(out=stg[si_][:], in_=pbank[pbn][:, :], func=AF.Copy), [b_pb[pbn]], [b_stg[si_]])
                            else:
                                kb.op("act", lambda pbn=pbn: E["act"].activation(out=sgf[:], in_=pbank[pbn][:, :], func=AF.Sigmoid), [b_pb[pbn]], [b_sgf])
                                kb.op("dve", lambda si_=si_, idx=idx: E["dve"].tensor_tensor(stg[si_][:], sgf[:], hgb0[:, idx * 512:(idx + 1) * 512], ALU.mult), [b_sgf, b_hgb0], [b_stg[si_]])
                            dst = vtm_d if kind == "v" else otm_d
                            kb.dma("sp", dst[ti * 128:(ti + 1) * 128, idx * 512:(idx + 1) * 512], stg[si_][:], [b_stg[si_]], [], "stg%d" % si_)
            for si, ti in enumerate(tiles):
                pbn = next_pb()
                for k in range(16):
                    kb.op("pe", lambda k=k, pbn=pbn, si=si: E["pe"].matmul(pbank[pbn][:, 0:16], hxT[:, k, si * 128:(si + 1) * 128], wg[:, k, :], start=(k == 0), stop=(k == 15)),
                          [b_wg, b_hx[si]], [b_pb[pbn]], inc=(k == 15))
                kb.op("dve", lambda pbn=pbn, ti=ti: E["dve"].tensor_copy(gtm[:, ti, :], pbank[pbn][:, 0:16]), [b_pb[pbn]], [b_gtm])
        kb.barrier()

    with ExitStack() as s0b:
        for _ in stage0(s0b, list(range(8, 24)), 0):
            pass
        kb.barrier()

    with ExitStack() as sc:
        cvf = sb(sc, "cvf", [128, 16, 5], F32); b_cvf = Buf("cvf")
        dg = sb(sc, "dg", [128, 5, 128], BF16); b_dg = Buf("dg")
        pre = [sb(sc, "pre%d" % i, [128, NTT * 128], BF16) for i in range(2)]; b_pre = [Buf("pre0"), Buf("pre1")]
        cs_ = [sb(sc, "cs%d" % i, [128, 512], BF16) for i in range(3)]; b_cs = [Buf("cs%d" % i) for i in range(3)]
        kt_ = [sb(sc, "kt%d" % i, [128, 512], BF16) for i in range(2)]; b_kt = [Buf("kts0"), Buf("kts1")]
        kb.dma("sp", cvf[:], convfm_d, [], [b_cvf], "cvf")
        cit = 0
        ci = 0; pbc = 0; kti = 0
        for chn in range(16):
            isq = chn < 8
            ntok = 2560 if isq else NTT * 128
            p_ = pre[chn % 2]; bp = b_pre[chn % 2]
            kb.dma("sp", p_[:, 0:ntok], qkpre_d[chn, :, 0:ntok], [], [bp], "pre%d" % (chn % 2))
            for j in range(5):
                kb.op("dve", lambda j=j, chn=chn: E["dve"].tensor_scalar(dg[:, j, :], identb, cvf[:, chn, j:j + 1], None, ALU.mult), [b_cstb, b_cvf], [b_dg])
            segs = [(0, T, 0, 2048 if isq else T)]
            if not isq: segs.append((T, T + TC, T, T + TC))
            for (slo, shi, olo, ohi) in segs:
                for t0 in range(olo, ohi, 512):
                    n = min(512, ohi - t0)
                    pbn = 2 + (pbc % 4); pbc += 1
                    order = [2, 0, 1, 3, 4]
                    for oi, j in enumerate(order):
                        a = max(t0, slo - (j - 2)); b = min(t0 + n, shi - (j - 2))
                        kb.op("pe", lambda j=j, a=a, b=b, t0=t0, pbn=pbn, p_=p_, oi=oi: E["pe"].matmul(pbank[pbn][:, a - t0:b - t0], dg[:, j, :], p_[:, a + j - 2:b + j - 2], start=(oi == 0), stop=(oi == 4)),
                              [b_dg, bp], [b_pb[pbn]], inc=(oi == 4))
                    c_ = cs_[ci % 3]; bc = b_cs[ci % 3]; cname = "cs%d" % (ci % 3); ci += 1
                    kb.op("act", lambda pbn=pbn, n=n, c_=c_: E["act"].activation(out=c_[:, 0:n], in_=pbank[pbn][:, 0:n], func=AF.Silu), [b_pb[pbn]], [bc])
                    if isq:
                        kb.dma("sp", qT_d[chn, :, t0:t0 + n], c_[:, 0:n], [bc], [], cname)
                    else:
                        kb.dma("sp", kT_d[chn - 8, :, t0:t0 + n], c_[:, 0:n], [bc], [], cname)
                        nb = n // 128
                        hb = kti % 2
                        for i in range(nb):
                            kb.op("pe", lambda i=i, hb=hb, c_=c_: E["pe"].transpose(pbank_b[hb][:, i * 128:(i + 1) * 128], c_[:, i * 128:(i + 1) * 128], identb),
                                  [bc, b_cstb], [b_pbb[hb]], inc=(i == nb - 1))
                        k_ = kt_[kti % 2]; bk = b_kt[kti % 2]; kname = "kts%d" % (kti % 2); kti += 1
                        kb.op("dve", lambda hb=hb, k_=k_, n=n: E["dve"].tensor_copy(k_[:, 0:n], pbank_b[hb][:, 0:n]), [b_pbb[hb]], [bk])
                        kb.dma("sp", ktm_d[t0:t0 + n, (chn - 8) * 128:(chn - 7) * 128].rearrange("(i p) c -> p i c", p=128),
                               k_[:, 0:n].rearrange("p (i c) -> p i c", c=128), [bk], [], kname)
        for i in range(16):
            kb.dma("pool", w1b_d[i * 128:(i + 1) * 128, :], w1_d[i * 128:(i + 1) * 128, :], [], [b_w1b], "wcast")
        for i in range(16):
            kb.dma("pool", w2b_d[i * 512:(i + 1) * 512, :], w2_d[i * 512:(i + 1) * 512, :], [], [b_w2b], "wcast")
        kb.barrier()

    with ExitStack() as sd:
        gb = sb(sd, "gb", [128, 16], F32); b_gb = Buf("gb")
        z = sb(sd, "z", [128, NTT, 16], F32); b_z = Buf("z")
        lf = sb(sd, "lf", [128, 2, NTT, 4], F32); b_lf = Buf("lf")
        bb = sb(sd, "bb", [128, 2, NTT, 4], F32); b_bb = Buf("bb")
        kb.dma("sp", gb[:], gateb_d, [], [b_gb], "gb")
        for ti in range(NTT):
            kb.op("dve", lambda ti=ti: E["dve"].tensor_tensor(z[:, ti, :], gtm[:, ti, :], gb[:], ALU.add), [b_gtm, b_gb], [b_z])
        for d in range(2):
            kb.op("act", lambda d=d: E["act"].activation(out=lf[:, d], in_=z[:, :, d * 8 + 4:d * 8 + 8], func=AF.Exp, scale=-1.0), [b_z], [b_lf])
        kb.op("act", lambda: E["act"].activation(out=lf[:], in_=lf[:], func=AF.Ln, bias=1.0), [b_lf], [b_lf])
        kb.op("dve", lambda: E["dve"].tensor_scalar(lf[:], lf[:], -1.0, None, ALU.mult), [b_lf], [b_lf])
        for d in range(2):
            tri = triA if d == 0 else triB
            kb.op("pe", lambda d=d, tri=tri: E["pe"].matmul(pbank[0][:, d * 136:(d + 1) * 136], tri, lf[:, d].rearrange("p t c -> p (t c)"), start=True, stop=True), [b_lf, b_cst], [b_pb[0]])
            kb.op("pe", lambda d=d: E["pe"].matmul(pbank[1][:, d * 136:(d + 1) * 136], onesf, lf[:, d].rearrange("p t c -> p (t c)"), start=True, stop=True), [b_lf, b_cst], [b_pb[1]])
        kb.op("dve", lambda: E["dve"].tensor_copy(bb[:].rearrange("p d t c -> p (d t c)"), pbank[0][:, 0:272]), [b_pb[0]], [b_bb])
        for d in range(2):
            kb.op("dve", lambda d=d: E["dve"].tensor_tensor(wS[:, :, d * 4:d * 4 + 4], z[:, :, d * 8:d * 8 + 4], bb[:, d], ALU.subtract), [b_z, b_bb], [b_wS])
            kb.op("act", lambda d=d: E["act"].activation(out=rhoS[:, :, d * 4:d * 4 + 4], in_=bb[:, d], func=AF.Exp, scale=-1.0, bias=float(np.log(16.0))), [b_bb], [b_rhoS])
            kb.op("act", lambda d=d: E["act"].activation(out=decS[:, :, d * 4:d * 4 + 4], in_=pbank[1][:, d * 136:(d + 1) * 136].rearrange("p (t c) -> p t c", c=4), func=AF.Exp), [b_pb[1]], [b_decS])
        kb.op("act", lambda: E["act"].activation(out=wS[:], in_=wS[:], func=AF.Exp), [b_wS], [b_wS])
        if debug:
            kb.dma("sp", dbg["gates"], gtm[:], [b_gtm], [], "dbggates")
        kb.barrier()
    ph1.close()
    ph2 = ExitStack()
    qbk = [ps(ph2, "qb%d" % i, [128, 512], F32) for i in range(2)]; b_qbk = [Buf("qb0"), Buf("qb1")]
    sbk = [ps(ph2, "sbk%d" % i, [128, 2, 512], F32) for i in range(2)]; b_sbk = [Buf("sbk0"), Buf("sbk1")]
    fbk = ps(ph2, "fbk", [128, 512], F32); b_fbk = Buf("fbk")
    pbank_b = [ps(ph2, "qbb0", [128, 1024], BF16)]; b_pbb = [Buf("qbb0")]

    with ExitStack() as se:
        qTh = sb(se, "qTh", [128, 2, 2048], BF16); kTh = sb(se, "kTh", [128, 2, NTT * 128], BF16)
        ktmh = sb(se, "ktmh", [128, NTT, 256], BF16); vtmh = sb(se, "vtmh", [128, NTT, 256], BF16)
        b_q = Buf("hq"); b_k = Buf("hk"); b_kt = Buf("hkt"); b_v = Buf("hv")
        b_hAd = {}
        Sh = sb(se, "Sh", [128, 2, 2, 257], F32); Sb = sb(se, "Sb", [128, 2, 2, 2, 257], BF16)
        b_Sh = [Buf("Sh0"), Buf("Sh1")]; b_Sb = [[Buf("Sb00"), Buf("Sb01")], [Buf("Sb10"), Buf("Sb11")]]
        Vp = [sb(se, "Vp%d" % i, [128, 257], BF16) for i in range(4)]; b_Vp = [Buf("Vp%d" % i) for i in range(4)]
        STm = [sb(se, "STm%d" % i, [128, 128], BF16) for i in range(2)]; b_STm = [Buf("STm0"), Buf("STm1")]
        sm = [sb(se, "sm%d" % i, [128, 8], F32) for i in range(2)]; b_sm = [Buf("sm0"), Buf("sm1")]
        hAst = [sb(se, "hAst%d" % i, [128, 256], F32) for i in range(2)]; b_hAst = [Buf("hAst0"), Buf("hAst1")]
        hAld = [sb(se, "hAld%d" % i, [128, 256], F32) for i in range(2)]; b_hAld = [Buf("hAld0"), Buf("hAld1")]
        ot = [sb(se, "ot%d" % i, [128, 256], BF16) for i in range(2)]; b_ot = [Buf("ot0"), Buf("ot1")]
        hh = sb(se, "hh", [128, 256], F32); b_hh = Buf("hh")
        ym = sb(se, "ym", [128, 256], BF16); b_ym = Buf("ym")
        junk2 = sb(se, "junk2", [128, 256], BF16); b_junk2 = Buf("junk2")
        yts = [sb(se, "yts%d" % i, [128, 2, 128], BF16) for i in range(2)]; b_yts = [Buf("yts0"), Buf("yts1")]
        Asb = sb(se, "Asb", [128, NTS, 1024], BF16); b_A = Buf("Asb")
        tc_ = [sb(se, "tc%d" % i, [128, 32, 128], BF16) for i in range(2)]; ts_ = [sb(se, "ts%d" % i, [128, 32, 128], BF16) for i in range(2)]
        b_tc = [Buf("tc0"), Buf("tc1")]; b_ts = [Buf("ts0"), Buf("ts1")]
        yst = [sb(se, "yst%d" % i, [128, 512], BF16) for i in range(2)]; b_yst = [Buf("yst0"), Buf("yst1")]
        yst2 = [sb(se, "ystb%d" % i, [128, 4, 128], BF16) for i in range(2)]; b_yst2 = [Buf("ystb0"), Buf("ystb1")]

        def gen_F():
            it = 0
            for pas in range(2):
                for q4 in range(4):
                    kb.dma("sp", Asb[:, q4 * 8:(q4 + 1) * 8, :], A_d[pas, q4 * 8:(q4 + 1) * 8].rearrange("t p c -> p t c"), [], [b_A], "Asb")
                for kch in range(16):
                    i2 = it % 2; it += 1
                    kb.dma("sp", tc_[i2][:], tabc_d[kch], [], [b_tc[i2]], "tc%d" % i2)
                    kb.dma("sp", ts_[i2][:], tabs_d[kch], [], [b_ts[i2]], "ts%d" % i2)
                    for tt in range(32):
                        kb.op("pe", lambda tt=tt, i2=i2: E["pe"].matmul(fbk[:, :], tc_[i2][:, tt, :], Asb[:, tt, 0:512], start=(tt == 0), stop=False), [b_tc[i2], b_A], [b_fbk], inc=False)
                        kb.op("pe", lambda tt=tt, i2=i2: E["pe"].matmul(fbk[:, :], ts_[i2][:, tt, :], Asb[:, tt, 512:1024], start=False, stop=(tt == 31)), [b_ts[i2], b_A], [b_fbk], inc=(tt == 31))
                        if tt % 4 == 3 and tt != 31:
                            yield
                    kb.op("act", lambda i2=i2: E["act"].activation(out=yst[i2][:], in_=fbk[:, :], func=AF.Copy), [b_fbk], [b_yst[i2]])
                    yield
                    for blk in range(4):
                        kb.op("pe", lambda blk=blk, i2=i2: E["pe"].transpose(pbank_b[0][:, blk * 128:(blk + 1) * 128], yst[i2][:, blk * 128:(blk + 1) * 128], identb), [b_yst[i2], b_cstb], [b_pbb[0]], inc=(blk == 3))
                    kb.op("dve", lambda i2=i2: E["dve"].tensor_copy(yst2[i2][:], pbank_b[0][:, 0:512].rearrange("p (b t) -> p b t", b=4)), [b_pbb[0]], [b_yst2[i2]])
                    kb.dma("pool", yT_d[kch, :, pas * 4:pas * 4 + 4, :], yst2[i2][:], [b_yst2[i2]], [], "ystb%d" % i2)
                    yield

        genF = gen_F()
        vpi = [0]; yti = [0]

        def prefetch_B(hd, ti):
            hl = hAld[ti % 2]; bhl = b_hAld[ti % 2]
            kb.dma("sp", hl[:], hA_d[ti * 128:(ti + 1) * 128, hd * 256:(hd + 1) * 256], [b_hAd[(hd, ti)]], [bhl], "hAld%d" % (ti % 2))
            o_ = ot[ti % 2]; bo = b_ot[ti % 2]
            kb.dma("sp", o_[:], otm_d[ti * 128:(ti + 1) * 128, hd * 256:(hd + 1) * 256], [], [bo], "ot%d" % (ti % 2))

        for hd in range(4):
            kb.dma("sp", qTh[:], qT_d[hd * 2:hd * 2 + 2].rearrange("c p t -> p c t"), [], [b_q], "hd_q")
            kb.dma("sp", kTh[:], kT_d[hd * 2:hd * 2 + 2].rearrange("c p t -> p c t"), [], [b_k], "hd_k")
            kb.dma("sp", ktmh[:], ktm_d[:, hd * 256:(hd + 1) * 256].rearrange("(i p) c -> p i c", p=128), [], [b_kt], "hd_kt")
            kb.dma("sp", vtmh[:], vtm_d[:, hd * 256:(hd + 1) * 256].rearrange("(i p) c -> p i c", p=128), [], [b_v], "hd_v")
            for d in range(2):
                kb.op("dve", lambda d=d: E["dve"].memset(Sh[:, d], 0.0), [], [b_Sh[d]])
            schedA = [(32, False), (33, False)] + [(c, True) for c in range(16)]
            schedB = [(33, False), (32, False)] + [(c, False) for c in range(31, 15, -1)] + [(c, True) for c in range(15, -1, -1)]
            steps = []
            for ib in range(18):
                steps.append((1, ib, schedB[ib])); steps.append((0, ib, schedA[ib]))
            for ib in range(18, 34):
                steps.append((1, ib, schedB[ib]))
            prevd = [None, None]
            ctxs = []
            for (d, idx, (ti, full)) in steps:
                col = d * 4 + hd
                dcol = decS[:, ti, col:col + 1]
                pdcol = prevd[d] if prevd[d] is not None else dcol
                prevd[d] = dcol
                ctxs.append(dict(d=d, idx=idx, ti=ti, full=full, wcol=wS[:, ti, col:col + 1], rcol=rhoS[:, ti, col:col + 1], pdcol=pdcol))

            def chain(c):
                d = c["d"]; ti = c["ti"]; full = c["full"]; wcol = c["wcol"]; pdcol = c["pdcol"]; par = c["idx"] % 2
                vi = vpi[0] % 4; vpi[0] += 1
                V = Vp[vi]; bV = b_Vp[vi]
                c["V"] = V; c["bV"] = bV
                kb.op("act", lambda: E["act"].activation(out=V[:, 0:256], in_=vtmh[:, ti, :], func=AF.Copy, scale=wcol), [b_v, b_wS], [bV])
                kb.op("dve", lambda: E["dve"].tensor_copy(V[:, 256:257], wcol), [b_wS], [bV])
                sk = sbk[d]; bsk = b_sbk[d]
                if full:
                    kb.op("act", lambda: E["act"].activation(out=Sb[:, d, par], in_=Sh[:, d], func=AF.Copy, scale=pdcol), [b_Sh[d], b_decS], [b_Sb[d][par]])
                for dc in range(2):
                    kb.op("pe", lambda dc=dc: E["pe"].matmul(sk[:, dc, 0:257], ktmh[:, ti, dc * 128:(dc + 1) * 128], V[:], start=True, stop=True), [b_kt, bV], [bsk], inc=(dc == 1))
                kb.op("dve", lambda: E["dve"].scalar_tensor_tensor(Sh[:, d], Sh[:, d], pdcol, sk[:, :, 0:257], ALU.mult, ALU.add), [b_Sh[d], bsk, b_decS], [b_Sh[d]])

            def output(c):
                d = c["d"]; ti = c["ti"]; rcol = c["rcol"]; par = c["idx"] % 2; V = c["V"]; bV = c["bV"]
                qb_ = qbk[d]; bqb = b_qbk[d]
                tcs = slice(ti * 128, (ti + 1) * 128)
                if d == 1:
                    if ti == 15: prefetch_B(hd, 15)
                    if ti > 0: prefetch_B(hd, ti - 1)
                for dc in range(2):
                    kb.op("pe", lambda dc=dc: E["pe"].matmul(qb_[:, 0:128], kTh[:, dc, tcs], qTh[:, dc, tcs], start=(dc == 0), stop=(dc == 1)),
                          [b_q, b_k], [bqb], inc=(dc == 1))
                msk = maskA if d == 0 else maskB
                kb.op("dve", lambda: E["dve"].tensor_tensor(STm[d][:], qb_[:, 0:128], msk, ALU.mult), [bqb, b_cstb], [b_STm[d]])
                for dc in range(2):
                    kb.op("pe", lambda dc=dc: E["pe"].matmul(qb_[:, 128:385], qTh[:, dc, tcs], Sb[:, d, par, dc, :], start=(dc == 0), stop=False),
                          [b_q, b_Sb[d][par]], [bqb], inc=False)
                kb.op("pe", lambda: E["pe"].matmul(qb_[:, 128:385], STm[d][:], V[:], start=False, stop=True), [b_STm[d], bV], [bqb])
                s_ = sm[d]; bs = b_sm[d]
                kb.op("act", lambda: E["act"].activation(out=s_[:, 5:6], in_=qb_[:, 384:385], func=AF.Abs), [bqb], [bs])
                kb.op("dve", lambda: E["dve"].tensor_tensor(s_[:, 0:1], s_[:, 5:6], rcol, ALU.max), [bs, b_rhoS], [bs])
                kb.op("dve", lambda: E["dve"].reciprocal(s_[:, 1:2], s_[:, 0:1]), [bs], [bs])
                if d == 0:
                    ha = hAst[ti % 2]; bha = b_hAst[ti % 2]
                    kb.op("act", lambda: E["act"].activation(out=ha[:], in_=qb_[:, 128:384], func=AF.Copy, scale=s_[:, 1:2]), [bqb, bs], [bha])
                    kb.dma("pool", hA_d[ti * 128:(ti + 1) * 128, hd * 256:(hd + 1) * 256], ha[:], [bha], [b_hAd.setdefault((hd, ti), Buf("hAd"))], "hAst%d" % (ti % 2))
                else:
                    hl = hAld[ti % 2]; bhl = b_hAld[ti % 2]
                    o_ = ot[ti % 2]; bo = b_ot[ti % 2]
                    kb.op("dve", lambda: E["dve"].scalar_tensor_tensor(hh[:], qb_[:, 128:384], s_[:, 1:2], hl[:], ALU.mult, ALU.add), [bqb, bs, bhl], [b_hh])
                    kb.op("act", lambda: E["act"].activation(out=junk2[:], in_=hh[:], func=AF.Square, accum_out=s_[:, 2:3]), [b_hh], [b_junk2, bs])
                    kb.op("act", lambda: E["act"].activation(out=s_[:, 3:4], in_=s_[:, 2:3], func=AF.Ln, scale=1.0 / 256, bias=EPS), [bs], [bs])
                    kb.op("act", lambda: E["act"].activation(out=s_[:, 4:5], in_=s_[:, 3:4], func=AF.Exp, scale=-0.5), [bs], [bs])
                    kb.op("dve", lambda: E["dve"].scalar_tensor_tensor(ym[:], hh[:], s_[:, 4:5], o_[:], ALU.mult, ALU.mult), [b_hh, bs, bo], [b_ym])
                    for blk in range(2):
                        kb.op("pe", lambda blk=blk: E["pe"].transpose(pbank_b[0][:, blk * 128:(blk + 1) * 128], ym[:, blk * 128:(blk + 1) * 128], identb), [b_ym, b_cstb], [b_pbb[0]], inc=(blk == 1))
                    yi = yti[0] % 2; yti[0] += 1
                    kb.op("act", lambda: E["act"].activation(out=yts[yi][:], in_=pbank_b[0][:, 0:256].rearrange("p (b t) -> p b t", b=2), func=AF.Copy), [b_pbb[0]], [b_yts[yi]])
                    kb.dma("pool", yT_d[ti, :, 8 + hd * 2:10 + hd * 2, :], yts[yi][:], [b_yts[yi]], [], "yts%d" % yi)

            ns = len(ctxs)
            for i in range(ns + 1):
                if i < ns:
                    chain(ctxs[i])
                    next(genF, None)
                if i >= 1 and ctxs[i - 1]["full"]:
                    output(ctxs[i - 1])
                    next(genF, None)
        for _ in genF:
            pass
        kb.barrier()
    ph2.close()
    ph3 = ExitStack()
    pbank = [ps(ph3, "rb%d" % i, [128, 512], F32) for i in range(6)]
    pbank_b = [ps(ph3, "rbb%d" % i, [128, 1024], BF16) for i in range(2)]
    b_pb = [Buf("rb%d" % i) for i in range(6)]; b_pbb = [Buf("rbb0"), Buf("rbb1")]

    wout_v = wout_d.rearrange("(k p) n -> p k n", p=128)
    with ExitStack() as sg1:
        wo = sb(sg1, "wo", [128, 16, D], BF16); b_wo = Buf("wo")
        gt1 = sb(sg1, "gt1", [128, D], F32); b_gt1 = Buf("gt1")
        xt = [sb(sg1, "gxt%d" % i, [128, D], F32) for i in range(2)]; b_xt = [Buf("gxt0"), Buf("gxt1")]
        pt = [sb(sg1, "gpt%d" % i, [128, D], F32) for i in range(2)]; b_pt = [Buf("gpt0"), Buf("gpt1")]
        xn = [sb(sg1, "gxn%d" % i, [128, D], BF16) for i in range(2)]; b_xn = [Buf("gxn0"), Buf("gxn1")]
        yTt = [sb(sg1, "yTt%d" % i, [128, 16, 128], BF16) for i in range(2)]; b_yTt = [Buf("yTt0"), Buf("yTt1")]
        st = sb(sg1, "gst", [128, 2, 4], F32); b_st = [Buf("gst0"), Buf("gst1")]
        h2s = [sb(sg1, "h2s%d" % i, [128, 16, 128], BF16) for i in range(2)]; b_h2s = [Buf("h2s0"), Buf("h2s1")]
        kb.dma("sp", gt1[:], gt_d[0], [b_gtd[0]], [b_gt1], "gt1")
        for q4 in range(4):
            kb.dma("pool", wo[:, :, q4 * 512:(q4 + 1) * 512], wout_v[:, :, q4 * 512:(q4 + 1) * 512], [], [b_wo], "wo")
        for k in range(16):
            kb.op("dve", lambda k=k: E["dve"].tensor_tensor(wo[:, k, :], wo[:, k, :], gt1[:], ALU.mult), [b_wo, b_gt1], [b_wo])

        def g1_loads(ti):
            i2 = ti % 2
            kb.dma("sp", yTt[i2][:], yT_d[ti], [], [b_yTt[i2]], "yTt%d" % i2)
            kb.dma("sp", xt[i2][:], x_d[ti * 128:(ti + 1) * 128, :], [], [b_xt[i2]], "gxt%d" % i2)
            kb.dma("sp", pt[i2][:], pos_d[ti * 128:(ti + 1) * 128, :], [], [b_pt[i2]], "gpt%d" % i2)

        def g1_front(ti):
            i2 = ti % 2
            xb = xt[i2]; bxb = b_xt[i2]; pb_ = pt[i2]; bpb = b_pt[i2]; yt_ = yTt[i2]; byt = b_yTt[i2]
            kb.op("dve", lambda: E["dve"].tensor_tensor(xb[:], xb[:], pb_[:], ALU.add), [bxb, bpb], [bxb])
            for fc in range(4):
                pbn = 2 + fc
                for k in range(16):
                    kb.op("pe", lambda k=k, fc=fc, pbn=pbn: E["pe"].matmul(pbank[pbn][:, :], yt_[:, k, :], wo[:, k, fc * 512:(fc + 1) * 512], start=(k == 0), stop=(k == 15)),
                          [byt, b_wo], [b_pb[pbn]], inc=(k == 15))
                kb.op("dve", lambda fc=fc, pbn=pbn: E["dve"].tensor_tensor(xb[:, fc * 512:(fc + 1) * 512], pbank[pbn][:, :], xb[:, fc * 512:(fc + 1) * 512], ALU.add), [b_pb[pbn], bxb], [bxb])
            kb.dma("pool", x1_d[ti * 128:(ti + 1) * 128, :], xb[:], [bxb], [], "gxs%d" % i2)
            if debug:
                kb.dma("pool", dbg["x1"][ti * 128:(ti + 1) * 128, :], xb[:], [bxb], [], "gxs%d" % i2)
            kb.op("act", lambda: E["act"].activation(out=xn[i2][:], in_=xb[:], func=AF.Square, accum_out=st[:, i2, 0:1]), [bxb], [b_xn[i2], b_st[i2]])
            kb.op("act", lambda: E["act"].activation(out=st[:, i2, 1:2], in_=st[:, i2, 0:1], func=AF.Ln, scale=1.0 / D, bias=EPS), [b_st[i2]], [b_st[i2]])
            kb.op("act", lambda: E["act"].activation(out=st[:, i2, 2:3], in_=st[:, i2, 1:2], func=AF.Exp, scale=-0.5), [b_st[i2]], [b_st[i2]])
            kb.op("act", lambda: E["act"].activation(out=xn[i2][:], in_=xb[:], func=AF.Copy, scale=st[:, i2, 2:3]), [bxb, b_st[i2]], [b_xn[i2]])

        def g1_back(ti):
            i2 = ti % 2
            h2 = h2s[i2]; bh2 = b_h2s[i2]
            for hb in range(2):
                for j in range(8):
                    jj = hb * 8 + j
                    kb.op("pe", lambda jj=jj, j=j, hb=hb: E["pe"].transpose(pbank_b[hb][:, j * 128:(j + 1) * 128], xn[i2][:, jj * 128:(jj + 1) * 128], identb), [b_xn[i2], b_cstb], [b_pbb[hb]], inc=(j == 7))
                for j in range(8):
                    jj = hb * 8 + j
                    kb.op("dve", lambda jj=jj, j=j, hb=hb: E["dve"].tensor_scalar(h2[:, jj, :], pbank_b[hb][:, j * 128:(j + 1) * 128], g2[:, jj:jj + 1], modfm[:, 48 + jj, 0:1], ALU.mult, ALU.add),
                          [b_pbb[hb], b_g2, b_modfm], [bh2])
            kb.dma("pool", h2T_d[ti // 4, :, :, (ti % 4) * 128:(ti % 4 + 1) * 128], h2[:], [bh2], [], "h2s%d" % i2)

        g1_loads(0)
        for ti in range(OWN + 1):
            if ti + 1 < OWN: g1_loads(ti + 1)
            if ti < OWN: g1_front(ti)
            if ti >= 1: g1_back(ti - 1)
        kb.barrier()

    w1_v = w1b_d.rearrange("(k p) n -> p k n", p=128)
    w2_v = w2b_d.rearrange("(k p) n -> p k n", p=128)
    with ExitStack() as sg2:
        h2b = sb(sg2, "h2b", [128, 16, 512], BF16); b_h2b = Buf("h2b")
        hid = sb(sg2, "hid", [128, 64, 512], BF16); b_hid = [Buf("hid%d" % i) for i in range(16)]
        w1p = [sb(sg2, "w1p%d" % i, [128, 16, 512], BF16) for i in range(2)]; b_w1p = [Buf("w1p0"), Buf("w1p1")]
        w2p = [sb(sg2, "w2p%d" % i, [128, 8, 512], BF16) for i in range(3)]; b_w2p = [Buf("w2p%d" % i) for i in range(3)]
        x2 = [sb(sg2, "x2_%d" % i, [128, D], F32) for i in range(4)]; b_x2 = [Buf("x2_%d" % i) for i in range(4)]
        gf = sb(sg2, "gf", [128, D], F32); b_gf = Buf("gf")
        gt2 = sb(sg2, "gt2", [128, D], F32); b_gt2 = Buf("gt2")
        rl = [sb(sg2, "rl%d" % i, [128, 512], F32) for i in range(2)]; b_rl = [Buf("rl0"), Buf("rl1")]
        st = sb(sg2, "fst", [128, 4], F32); b_st = Buf("fst")
        kb.dma("sp", gf[:], gfin_d, [], [b_gf], "gf")
        kb.dma("sp", gt2[:], gt_d[1], [b_gtd[1]], [b_gt2], "gt2")
        w1i = 0; w2i = 0; rli = 0
        for tb in range(4):
            kb.dma("sp", h2b[:], h2T_d[tb], [], [b_h2b], "h2b")
            for i in range(4):
                ti = tb * 4 + i
                kb.dma("sp", x2[i][:], x1_d[ti * 128:(ti + 1) * 128, :], [], [b_x2[i]], "x2_%d" % i)
            for pj in range(16):
                w = w1p[w1i % 2]; bw = b_w1p[w1i % 2]
                kb.dma("sp", w[:], w1_v[:, :, pj * 512:(pj + 1) * 512], [b_w1b], [bw], "w1p%d" % (w1i % 2)); w1i += 1
                for jc in range(4):
                    pbn = rli % 2
                    for k in range(16):
                        kb.op("pe", lambda k=k, jc=jc, w=w, pbn=pbn: E["pe"].matmul(pbank[pbn][:, :], w[:, k, jc * 128:(jc + 1) * 128], h2b[:, k, :], start=(k == 0), stop=(k == 15)), [bw, b_h2b], [b_pb[pbn]], inc=(k == 15))
                    r_ = rl[rli % 2]; br = b_rl[rli % 2]; rli += 1
                    kb.op("act", lambda pbn=pbn, r_=r_: E["act"].activation(out=r_[:], in_=pbank[pbn][:, :], func=AF.Relu), [b_pb[pbn]], [br])
                    kb.op("dve", lambda r_=r_, pj=pj, jc=jc: E["dve"].tensor_tensor(hid[:, pj * 4 + jc, :], r_[:], r_[:], ALU.mult), [br], [b_hid[pj]])
            for fc in range(4):
                for jq in range(8):
                    w = w2p[w2i % 3]; bw = b_w2p[w2i % 3]
                    kb.dma("sp", w[:], w2_v[:, jq * 8:(jq + 1) * 8, fc * 512:(fc + 1) * 512], [b_w2b], [bw], "w2p%d" % (w2i % 3)); w2i += 1
                    for i in range(4):
                        pbn = 2 + i
                        for k in range(8):
                            jc = jq * 8 + k
                            kb.op("pe", lambda k=k, jc=jc, i=i, w=w, pbn=pbn, jq=jq: E["pe"].matmul(pbank[pbn][:, :], hid[:, jc, i * 128:(i + 1) * 128], w[:, k, :], start=(jq == 0 and k == 0), stop=(jq == 7 and k == 7)),
                                  [bw, b_hid[jc // 4]], [b_pb[pbn]], inc=(k == 7))
                for i in range(4):
                    pbn = 2 + i
                    r_ = rl[rli % 2]; br = b_rl[rli % 2]; rli += 1
                    kb.op("dve", lambda fc=fc, pbn=pbn, r_=r_: E["dve"].tensor_tensor(r_[:], pbank[pbn][:, :], gt2[:, fc * 512:(fc + 1) * 512], ALU.mult), [b_pb[pbn], b_gt2], [br])
                    kb.op("dve", lambda fc=fc, i=i, r_=r_: E["dve"].tensor_tensor(x2[i][:, fc * 512:(fc + 1) * 512], r_[:], x2[i][:, fc * 512:(fc + 1) * 512], ALU.add), [br, b_x2[i]], [b_x2[i]])
            for i in range(4):
                ti = tb * 4 + i
                kb.op("act", lambda i=i: E["act"].activation(out=hid[:, 0:4, :].rearrange("p a b -> p (a b)"), in_=x2[i][:], func=AF.Square, accum_out=st[:, 0:1]), [b_x2[i]], [b_hid[0], b_st])
                kb.op("act", lambda: E["act"].activation(out=st[:, 1:2], in_=st[:, 0:1], func=AF.Ln, scale=1.0 / D, bias=EPS), [b_st], [b_st])
                kb.op("act", lambda: E["act"].activation(out=st[:, 2:3], in_=st[:, 1:2], func=AF.Exp, scale=-0.5), [b_st], [b_st])
                kb.op("dve", lambda i=i: E["dve"].scalar_tensor_tensor(x2[i][:], x2[i][:], st[:, 2:3], gf[:], ALU.mult, ALU.mult), [b_x2[i], b_st, b_gf], [b_x2[i]])
                kb.dma("pool", out_d[ti * 128:(ti + 1) * 128, :], x2[i][:], [b_x2[i]], [], "x2s_%d" % i)
        kb.final_wait()
    ph3.close()
    es.close()
    return nc


def _pos_table():
    quarter = D // 4
    omega = (1.0 / (np.float32(10000.0) ** (np.arange(quarter, dtype=np.float32) / np.float32(quarter)))).astype(np.float32)
    ar = (np.arange(64, dtype=np.float32)[:, None] * omega).astype(np.float32)
    emb = np.concatenate([np.sin(ar), np.cos(ar)], -1).astype(np.float32)
    row = np.broadcast_to(emb[:, None, :], (64, 64, 1024))
    col = np.broadcast_to(emb[None, :, :], (64, 64, 1024))
    return np.ascontiguousarray(np.concatenate([row, col], -1).reshape(T, D).astype(np.float32))


def _consts():
    c = {}
    c["pos"] = _pos_table()
    j = np.arange(256)[:, None].astype(np.int64); m = np.arange(256)[None, :].astype(np.int64)
    ang = 2.0 * np.pi * ((j * m) % 256) / 256.0
    cs = np.concatenate([np.cos(ang), np.sin(ang)], axis=1)
    c["csd"] = np.ascontiguousarray(cs.reshape(2, 128, 512).transpose(1, 0, 2)).astype(NPBF)
    s = np.arange(128)[:, None]; t = np.arange(128)[None, :]
    cst = np.zeros((128, 5, 128), np.float32)
    cst[:, 0, :] = (s <= t); cst[:, 1, :] = (s >= t); cst[:, 2, :] = (s == t); cst[:, 3, :] = 1.0
    c["cst"] = cst
    for fl in range(2):
        tp = np.arange(T, dtype=np.int64); ii = np.arange(2048, dtype=np.int64)
        to = (T - 1 - tp) if fl else tp
        ko = (T - 1 - ii) if fl else ii
        ang = 2.0 * np.pi * ((to[:, None] * ko[None, :]) % T) / T
        for nm, fn, sgn in (("tabc", np.cos, 1.0), ("tabs", np.sin, -1.0)):
            tab = (sgn * fn(ang) / 1024.0).astype(np.float32)
            tab = tab.reshape(32, 128, 16, 128).transpose(2, 1, 0, 3)
            c["%s%d" % (nm, fl)] = np.ascontiguousarray(tab).astype(NPBF)
    return c


def fm(v, n):
    return np.ascontiguousarray(v.reshape(n, 128).T.astype(np.float32))


def bc(v):
    return np.ascontiguousarray(np.broadcast_to(v.reshape(1, -1), (128, v.size)).astype(np.float32))


_CACHE = {}


def kernel(x, c, ctx, c_ctx, w_mod, b_mod, g_mix, g_mlp, w_in, conv_w, gate_b, head_g, w_out, w_mlp1, w_mlp2, g_final, _debug=False):
    x = np.asarray(x, np.float32); ctx = np.asarray(ctx, np.float32)
    if "c" not in _CACHE: _CACHE["c"] = _consts()
    C = _CACHE["c"]
    w_in0 = np.asarray(w_in[0], np.float32)
    perm = np.concatenate([np.arange(5120), 5120 + 8 + np.arange(8), 5120 + np.arange(8)])
    w_in1 = np.ascontiguousarray(w_in0[:, perm])
    gb0 = np.asarray(gate_b[0], np.float32).reshape(16); gb1 = np.asarray(gate_b[0], np.float32)[::-1].reshape(16)
    cw0 = np.asarray(conv_w[0], np.float32); cw1 = cw0[::-1]
    bm = np.asarray(b_mod[0], np.float32)
    shared = {
        "w_mod": np.ascontiguousarray(np.asarray(w_mod[0], np.float32)), "bmod_fm": fm(bm, 96),
        "bmod_bc": np.ascontiguousarray(np.stack([bc(bm[4096:6144]), bc(bm[10240:12288])], axis=1)),
        "gmix_fm": fm(np.asarray(g_mix[0], np.float32), 16), "gmlp_fm": fm(np.asarray(g_mlp[0], np.float32), 16),
        "headg_bc": bc(np.asarray(head_g[0], np.float32)), "gfin_bc": bc(np.asarray(g_final, np.float32)),
        "w_out": np.ascontiguousarray(np.asarray(w_out[0], np.float32)), "w_mlp1": np.ascontiguousarray(np.asarray(w_mlp1[0], np.float32)),
        "w_mlp2": np.ascontiguousarray(np.asarray(w_mlp2[0], np.float32)), "csd": C["csd"], "cst": C["cst"],
    }
    in_maps = []
    for core in range(8):
        b = core // 2; fl = core % 2
        xb = x[b][::-1] if fl else x[b]; cb = ctx[b][::-1] if fl else ctx[b]
        cc = np.stack([np.asarray(c[b], np.float32), np.asarray(c_ctx, np.float32)], axis=0)
        ccT = np.ascontiguousarray(cc.reshape(2, 16, 128).transpose(2, 1, 0))
        cw = cw1 if fl else cw0
        m = dict(shared)
        m.update({
            "x": np.ascontiguousarray(xb), "ctx": np.ascontiguousarray(cb), "pos": np.ascontiguousarray(C["pos"][::-1]) if fl else C["pos"],
            "ccT": ccT, "w_in": w_in1 if fl else w_in0,
            "conv_fm": np.ascontiguousarray(cw.reshape(5, 16, 128).transpose(2, 1, 0)), "gateb_bc": bc(gb1 if fl else gb0),
            "tabc": C["tabc%d" % fl], "tabs": C["tabs%d" % fl],
        })
        in_maps.append(m)
    key = "nc_dbg" if _debug else "nc"
    if key not in _CACHE: _CACHE[key] = build_program(debug=_debug)
    res = run_bass_kernel_spmd(_CACHE[key], in_maps, core_ids=list(range(8)))
    out = np.empty((4, T, D), np.float32)
    for core in range(8):
        b = core // 2; fl = core % 2
        o = np.asarray(res.results[core]["out"], np.float32)
        if fl: out[b, 2048:] = o[::-1]
        else: out[b, :2048] = o
    if _debug:
        return out, res
    return out
```
